# Optimizing a Trainium2 kernel written in Bass

```python
import jax, jax.numpy as jnp
from jax import lax
import numpy as np

D_MODEL = 1024
BATCH = 2
SEQ = 8192
DEPTH = 1

CHUNK = 64
D_MIX = D_MODEL
LRU_WIDTH = D_MIX // 2
LRU_HEADS = 8
LRU_HEAD_DIM = LRU_WIDTH // LRU_HEADS
CONV_WIDTH = 4
LRU_C = 8.0
POOL_WIDTH = D_MIX - LRU_WIDTH
POOL_WINDOWS = (2, 4, 8, 16)
POOL_GROUPS = len(POOL_WINDOWS)
POOL_GROUP_DIM = POOL_WIDTH // POOL_GROUPS
D_IN = 2 * LRU_WIDTH + POOL_WIDTH
D_FF = ((8 * D_MODEL // 3 + 255) // 256) * 256
EPS = 1e-6

kernel_name = "hybrid_rglru_multiscale_pool_swiglu"


def rmsnorm(x, g):
    xf = x.astype(jnp.float32)
    return xf * lax.rsqrt(jnp.mean(xf * xf, axis=-1, keepdims=True) + EPS) * g.astype(jnp.float32)


def causal_depthwise_conv(x, w, b):
    S = x.shape[1]
    xp = jnp.pad(x, ((0, 0), (CONV_WIDTH - 1, 0), (0, 0)))
    out = b
    for k in range(CONV_WIDTH):
        out = out + xp[:, k:k + S, :] * w[k]
    return out


def rg_lru(x, wa, ba, wx, bx, lam):
    B, S, _ = x.shape
    xh = x.reshape(B, S, LRU_HEADS, LRU_HEAD_DIM)
    r = jax.nn.sigmoid(jnp.einsum('bshi,hij->bshj', xh, wa).reshape(B, S, LRU_WIDTH) + ba)
    i = jax.nn.sigmoid(jnp.einsum('bshi,hij->bshj', xh, wx).reshape(B, S, LRU_WIDTH) + bx)
    log_a = -LRU_C * r * jax.nn.softplus(-lam)
    a = jnp.exp(log_a)
    mult = jnp.sqrt(jnp.maximum(-jnp.expm1(2.0 * log_a), 1e-12))
    b = mult * (i * x)

    def combine(left, right):
        a1, b1 = left
        a2, b2 = right
        return a1 * a2, a2 * b1 + b2

    _, h = lax.associative_scan(combine, (a, b), axis=1)
    return h


def trailing_mean(x, w):
    S = x.shape[1]
    cs = jnp.cumsum(jnp.pad(x, ((0, 0), (w, 0), (0, 0))), axis=1)
    window_sum = cs[:, w:, :] - cs[:, :S, :]
    count = jnp.minimum(jnp.arange(1, S + 1), w).astype(jnp.float32)[None, :, None]
    return window_sum / count


def pool_mixer(u, pool_w, pool_b, pool_scale):
    B, S, _ = u.shape
    ug = u.reshape(B, S, POOL_GROUPS, POOL_GROUP_DIM)
    pooled = jnp.stack(
        [trailing_mean(ug[:, :, g, :], w) - ug[:, :, g, :] for g, w in enumerate(POOL_WINDOWS)],
        axis=2)
    y = jnp.einsum('bsgc,gcd->bsgd', pooled, pool_w).reshape(B, S, POOL_WIDTH) + pool_b
    return y * pool_scale


def setup_inputs(seed: int = 0) -> dict:
    key = jax.random.key(seed)
    ks = jax.random.split(key, 24)
    f32 = jnp.float32
    nrm = lambda k, shape, fan_in: jax.random.normal(k, shape, f32) * (fan_in ** -0.5)
    gain = lambda k, shape: 1.0 + 0.02 * jax.random.normal(k, shape, f32)
    small = lambda k, shape: 0.01 * jax.random.normal(k, shape, f32)
    u = jax.random.uniform(ks[8], (DEPTH, LRU_WIDTH), f32, minval=0.9, maxval=0.999)
    a0 = u ** (1.0 / LRU_C)
    lam = jnp.log(a0) - jnp.log1p(-a0)
    return {
        "x": jax.random.normal(ks[0], (BATCH, SEQ, D_MODEL), f32),
        "norm_mix_g": gain(ks[1], (DEPTH, D_MODEL)),
        "w_in": nrm(ks[2], (DEPTH, D_MODEL, D_IN), D_MODEL),
        "conv_w": nrm(ks[3], (DEPTH, CONV_WIDTH, LRU_WIDTH), CONV_WIDTH),
        "conv_b": small(ks[4], (DEPTH, LRU_WIDTH)),
        "gate_a_w": nrm(ks[5], (DEPTH, LRU_HEADS, LRU_HEAD_DIM, LRU_HEAD_DIM), LRU_HEAD_DIM),
        "gate_a_b": small(ks[6], (DEPTH, LRU_WIDTH)),
        "gate_x_w": nrm(ks[7], (DEPTH, LRU_HEADS, LRU_HEAD_DIM, LRU_HEAD_DIM), LRU_HEAD_DIM),
        "gate_x_b": small(ks[9], (DEPTH, LRU_WIDTH)),
        "lru_lambda": lam,
        "pool_w": nrm(ks[10], (DEPTH, POOL_GROUPS, POOL_GROUP_DIM, POOL_GROUP_DIM), POOL_GROUP_DIM),
        "pool_b": small(ks[11], (DEPTH, POOL_WIDTH)),
        "pool_scale": gain(ks[12], (DEPTH, POOL_WIDTH)),
        "norm_lru_g": gain(ks[13], (DEPTH, LRU_WIDTH)),
        "norm_pool_g": gain(ks[14], (DEPTH, POOL_WIDTH)),
        "w_out": nrm(ks[15], (DEPTH, D_MIX, D_MODEL), D_MIX),
        "norm_ffn_g": gain(ks[16], (DEPTH, D_MODEL)),
        "ffn_w1": nrm(ks[17], (DEPTH, D_MODEL, D_FF), D_MODEL),
        "ffn_w3": nrm(ks[18], (DEPTH, D_MODEL, D_FF), D_MODEL),
        "ffn_w2": nrm(ks[19], (DEPTH, D_FF, D_MODEL), D_FF),
        "final_norm_g": gain(ks[20], (D_MODEL,)),
    }


def reference(x, norm_mix_g, w_in, conv_w, conv_b, gate_a_w, gate_a_b, gate_x_w, gate_x_b,
              lru_lambda, pool_w, pool_b, pool_scale, norm_lru_g, norm_pool_g, w_out,
              norm_ffn_g, ffn_w1, ffn_w3, ffn_w2, final_norm_g):
    out_dtype = x.dtype
    h_res = x.astype(jnp.float32)
    for l in range(DEPTH):
        h = rmsnorm(h_res, norm_mix_g[l])
        u = h @ w_in[l].astype(jnp.float32)
        u_lru = u[..., :LRU_WIDTH]
        u_gate = u[..., LRU_WIDTH:2 * LRU_WIDTH]
        u_pool = u[..., 2 * LRU_WIDTH:]
        xc = causal_depthwise_conv(u_lru, conv_w[l], conv_b[l])
        y_lru = rg_lru(xc, gate_a_w[l], gate_a_b[l], gate_x_w[l], gate_x_b[l], lru_lambda[l])
        y_lru = y_lru * jax.nn.gelu(u_gate, approximate=True)
        y_pool = pool_mixer(u_pool, pool_w[l], pool_b[l], pool_scale[l])
        y = jnp.concatenate([rmsnorm(y_lru, norm_lru_g[l]), rmsnorm(y_pool, norm_pool_g[l])], axis=-1)
        h_res = h_res + y @ w_out[l].astype(jnp.float32)
        h = rmsnorm(h_res, norm_ffn_g[l])
        ff = jax.nn.silu(h @ ffn_w1[l].astype(jnp.float32)) * (h @ ffn_w3[l].astype(jnp.float32))
        h_res = h_res + ff @ ffn_w2[l].astype(jnp.float32)
    return rmsnorm(h_res, final_norm_g).astype(out_dtype)
```

```python
import numpy as np
import concourse.bass as bass
import concourse.mybir as mybir
from concourse.bass_utils import run_bass_kernel_spmd

F32 = mybir.dt.float32
BF16 = mybir.dt.bfloat16
AF = mybir.ActivationFunctionType
ALU = mybir.AluOpType

NCORES = 8
D = 1024
T = 2048
HALO = 16
NB = 4
BN = 512
DFF = 2816
NG = 11
GW = 256
PRE_BLK = 12
NQ = PRE_BLK + NB
XCOLS = HALO + NQ * BN
EPS = 1e-6
WINS = (2, 4, 8, 16)

PC = {}
_o = 0
for _nm, _w in [("g_mix", 8), ("conv_w", 16), ("conv_b", 4), ("ba", 4), ("bx", 4), ("lam", 4), ("pool_b", 4),
                ("pool_s", 4), ("g_lru", 4), ("g_pool", 4), ("g_ffn", 8), ("g_fin", 8), ("invc", 64), ("pmask", 12)]:
    PC[_nm] = _o
    _o += _w
NPAR = _o

KB = 1024


class Res:
    __slots__ = ("name", "w", "r")

    def __init__(self, name):
        self.name = name
        self.w = None
        self.r = {}


class Prog:
    def __init__(self, nc):
        self.nc = nc
        self.eng = {"pe": nc.tensor, "act": nc.scalar, "dve": nc.vector, "pool": nc.gpsimd, "sp": nc.sync}
        self.sems = {}
        self.cnt = {}
        self.known = {e: {} for e in self.eng}
        self._stack = []
        for e in self.eng:
            self._mk(e)
        self.n_wait = 0
        self.n_ins = 0

    def _mk(self, key):
        cm = self.nc.semaphore("s_" + key)
        h = cm.__enter__()
        self._stack.append(cm)
        self.sems[key] = h
        self.cnt[key] = 0
        return h

    def close(self):
        for cm in reversed(self._stack):
            cm.__exit__(None, None, None)

    def _need(self, e, tok):
        if tok is None:
            return
        k, v = tok
        if k == "pe" and e == "pe":
            return
        if self.known[e].get(k, 0) < v:
            self.eng[e].wait_ge(self.sems[k], v)
            self.known[e][k] = v
            self.n_wait += 1

    def deps(self, e, reads, writes):
        for r in reads:
            self._need(e, r.w)
        for w in writes:
            self._need(e, w.w)
            for k, v in w.r.items():
                self._need(e, (k, v))

    def mark(self, tok, reads, writes):
        k, v = tok
        for r in reads:
            if r.r.get(k, 0) < v:
                r.r[k] = v
        for w in writes:
            w.w = tok
            w.r = {}

    def op(self, e, fn, reads=(), writes=(), signal=True):
        self.deps(e, reads, writes)
        ins = fn()
        self.n_ins += 1
        if signal:
            self.cnt[e] += 1
            ins.then_inc(self.sems[e], 1)
            tok = (e, self.cnt[e])
        else:
            tok = (e, self.cnt[e] + 1)
        self.mark(tok, reads, writes)
        return ins

    def dma(self, q, semkey, out, in_, reads=(), writes=(), **kw):
        if semkey not in self.sems:
            self._mk(semkey)
        self.deps(q, reads, writes)
        ins = self.eng[q].dma_start(out=out, in_=in_, **kw)
        self.cnt[semkey] += 16
        ins.then_inc(self.sems[semkey], 16)
        self.mark((semkey, self.cnt[semkey]), reads, writes)
        self.n_ins += 1
        return ins


class Rot:
    def __init__(self, views, name):
        self.v = views
        self.r = [Res(f"{name}{i}") for i in range(len(views))]
        self.i = -1

    def next(self):
        self.i = (self.i + 1) % len(self.v)
        return self.v[self.i], self.r[self.i]


STOP = None


def build_nc():
    nc = bass.Bass("TRN2", target_bir_lowering=False)
    dt_in = lambda name, shape: nc.dram_tensor(name, shape, F32, kind="ExternalInput").ap()
    xT = dt_in("xT", [128, 8, XCOLS])
    w_in_d = dt_in("w_in", [128, 8, 1536])
    w_out_d = dt_in("w_out", [128, 8, 1024])
    w1_d = dt_in("w1g", [NG, 128, 8, GW])
    w3_d = dt_in("w3g", [NG, 128, 8, GW])
    w2_d = dt_in("w2g", [NG, 128, 2, 1024])
    gates_d = dt_in("gates", [128, 2, 4, 128])
    poolw_d = dt_in("poolw", [128, 4, 128])
    par_d = dt_in("params", [128, NPAR])
    out_d = nc.dram_tensor("out", [128, 8, T], F32, kind="ExternalOutput").ap()

    ARENA = 204 * KB
    with (
        nc.sbuf_tensor("arena", [128, ARENA // 4], F32) as arena,
        nc.sbuf_tensor("misc", [128, 512], F32) as misc,
        nc.psum_tensor("ps0", [128, 512], F32) as ps0, nc.psum_tensor("ps1", [128, 512], F32) as ps1,
        nc.psum_tensor("ps2", [128, 512], F32) as ps2, nc.psum_tensor("ps3", [128, 512], F32) as ps3,
        nc.psum_tensor("ps4", [128, 512], F32) as ps4, nc.psum_tensor("ps5", [128, 512], F32) as ps5,
        nc.psum_tensor("ps6", [128, 512], F32) as ps6, nc.psum_tensor("ps7", [128, 512], F32) as ps7,
    ):
        P = Prog(nc)
        PS = [ps0, ps1, ps2, ps3, ps4, ps5, ps6, ps7]
        PSR = [Res(f"ps{i}") for i in range(8)]

        def V(off, dtype, shape):
            n = int(np.prod(shape))
            esz = 4 if dtype == F32 else 2
            nb = n * esz
            assert off % 4 == 0 and nb % 4 == 0 and off + nb <= ARENA, (off, nb)
            ap = arena[:, off // 4:(off + nb) // 4]
            if dtype != F32:
                ap = ap.bitcast(dtype)
            if len(shape) == 2:
                ap = ap.rearrange("p (a b) -> p a b", a=shape[0])
            elif len(shape) == 3:
                ap = ap.rearrange("p (a b c) -> p a b c", a=shape[0], b=shape[1])
            return ap

        R_OFF = 0
        RTILE = [[R_OFF + n * 24 * KB + i * 2 * KB for i in range(12)] for n in range(NB)]
        RRES = [[Res(f"R{n}_{i}") for i in range(12)] for n in range(NB)]
        W_IN_OFF = 96 * KB
        W_OUT_OFF = 120 * KB
        GATES_OFF = 136 * KB
        POOLW_OFF = 138 * KB
        ZERO_OFF = 139 * KB
        ONES_OFF = 141 * KB
        STR_OFF = 142 * KB
        T0_OFF = 185 * KB + 512

        w_in_sb = V(W_IN_OFF, BF16, [8, 1536])
        w_out_sb = V(W_OUT_OFF, BF16, [8, 1024])
        gates_sb = V(GATES_OFF, BF16, [2, 4, 128])
        poolw_sb = V(POOLW_OFF, BF16, [4, 128])
        zeros = V(ZERO_OFF, F32, [512])
        ones1024 = V(ONES_OFF, BF16, [128])
        ones512 = V(ONES_OFF + 256, BF16, [128])
        par = misc[:, 0:NPAR]
        DER = NPAR
        d_hba = misc[:, DER + 0:DER + 4]
        d_hbx = misc[:, DER + 4:DER + 8]
        d_hc1 = misc[:, DER + 8:DER + 12]
        d_pbs = misc[:, DER + 12:DER + 16]
        d_tmp = misc[:, DER + 16:DER + 20]
        hc = misc[:, DER + 20:DER + 24]
        pc = misc[:, DER + 24:DER + 28]
        ab = misc[:, DER + 28:DER + 36]
        hs = misc[:, DER + 36:DER + 40]
        hs_t = misc[:, DER + 40:DER + 44]
        gat = misc[:, DER + 48:DER + 48 + 64].rearrange("p (r f) -> p r f", r=8)
        assert DER + 48 + 64 <= 512
        R_par = Res("par"); R_der = Res("der"); R_hc = Res("hc"); R_pc = Res("pc"); R_ab = Res("ab")
        R_hs = Res("hs"); R_gat = Res("gat"); R_const = Res("const")
        R_win = [Res(f"win{i}") for i in range(4)]
        R_wout = [Res(f"wout{i}") for i in range(2)]
        R_gates = Res("gates"); R_poolw = Res("poolw")
        R_ccin = Res("ccin"); R_ccout = Res("ccout")

        def pcol(name, i=0, n=1):
            return par[:, PC[name] + i:PC[name] + i + n]

        o = STR_OFF
        xt_rot = Rot([V(RTILE[i // 2][6 + i % 2], F32, [512]) for i in range(8)], "xt")
        xt_rot.r = [RRES[i // 2][6 + i % 2] for i in range(8)]
        o += 4 * KB
        sq_rot = Rot([V(o + i * KB, BF16, [512]) for i in range(2)], "sq"); o += 2 * KB
        xg = V(o, BF16, [8, 512]); R_xg = [Res(f"xg{k}") for k in range(8)]; o += 8 * KB
        UW = HALO + BN
        u_lru = V(o, F32, [4, UW]); R_ulru = [Res(f"ulru{k}") for k in range(4)]; o += 4 * UW * 4
        u_pool = V(o, F32, [4, UW]); R_upool = [Res(f"upool{k}") for k in range(4)]; o += 4 * UW * 4
        rstd_rot = Rot([V(o + i * 2 * KB, F32, [512]) for i in range(2)], "rstd"); o += 4 * KB
        B_rot = Rot([V(o + i * 2 * KB, F32, [512]) for i in range(4)], "B"); o += 8 * KB
        assert o <= T0_OFF, (o, T0_OFF)
        o = T0_OFF
        E_rot = Rot([V(o + i * 2 * KB, F32, [512]) for i in range(4)], "E"); o += 8 * KB
        C_rot = Rot([V(RTILE[i][4], F32, [512]) for i in range(4)], "C")
        C_rot.r = [RRES[i][4] for i in range(4)]
        xcb_rot = Rot([V(RTILE[i][5], BF16, [512]) for i in range(4)], "xcb")
        xcb_rot.r = [RRES[i][5] for i in range(4)]
        SW = UW * 4
        S_rot = Rot([V(o + i * SW, F32, [UW]) for i in range(2)], "S"); o += 2 * SW
        pl_rot = Rot([V(o + i * KB, BF16, [512]) for i in range(2)], "pl"); o += 2 * KB
        ysq_rot = Rot([V(o + i * KB, BF16, [512]) for i in range(2)], "ysq"); o += 2 * KB
        XC7_OFF = o; o += 2 * KB
        assert o <= ARENA, (o, ARENA)
        xc_rot = Rot([V(W_OUT_OFF + i * 2 * KB, F32, [512]) for i in range(4)]
                     + [V(STR_OFF, F32, [512]), V(STR_OFF + 2 * KB, F32, [512]), V(ZERO_OFF, F32, [512]), V(XC7_OFF, F32, [512])], "xc")
        A_rot = Rot([V(W_OUT_OFF + 8 * KB + i * 2 * KB, F32, [512]) for i in range(4)], "A")
        F_views = [V(RTILE[n][10], F32, [512]) for n in range(NB)]
        F_res = [RRES[n][10] for n in range(NB)]
        yp_views = [V(RTILE[n][11], F32, [512]) for n in range(NB)]
        yp_res = [RRES[n][11] for n in range(NB)]

        def act(out, in_, func, reads, writes, scale=1.0, bias=None):
            kw = {} if bias is None else {"bias": bias}
            return P.op("act", lambda: nc.scalar.activation(out=out, in_=in_, func=func, scale=scale, **kw), reads, writes)

        def ts(e, out, in0, s1, op0, reads, writes, s2=None, op1=None):
            eng = nc.vector if e == "dve" else nc.gpsimd
            if op1 is None:
                return P.op(e, lambda: eng.tensor_scalar(out=out, in0=in0, scalar1=s1, scalar2=None, op0=op0), reads, writes)
            return P.op(e, lambda: eng.tensor_scalar(out=out, in0=in0, scalar1=s1, scalar2=s2, op0=op0, op1=op1), reads, writes)

        def tt(e, out, in0, in1, op, reads, writes):
            eng = nc.vector if e == "dve" else nc.gpsimd
            return P.op(e, lambda: eng.tensor_tensor(out=out, in0=in0, in1=in1, op=op), reads, writes)

        def stt(out, in0, scalar, in1, op0, op1, reads, writes):
            return P.op("dve", lambda: nc.vector.scalar_tensor_tensor(out=out, in0=in0, scalar=scalar, in1=in1, op0=op0, op1=op1), reads, writes)

        def mm(out, lhsT, rhs, start, stop, reads, writes, sig=False):
            return P.op("pe", lambda: nc.tensor.matmul(out, lhsT, rhs, start=start, stop=stop), reads, writes, signal=(stop or sig))

        P.dma("sp", "d_par", par, par_d, writes=[R_par])
        for i, c0_ in enumerate([0, 1024, 512]):
            P.dma("pool", f"d_win{i}", w_in_sb[:, :, c0_:c0_ + 512], w_in_d[:, :, c0_:c0_ + 512], writes=[R_win[i]], max_dma_last_dim=4096)
        P.dma("pool", "d_gates", gates_sb, gates_d, writes=[R_gates], max_dma_last_dim=4096)
        P.dma("pool", "d_poolw", poolw_sb, poolw_d, writes=[R_poolw], max_dma_last_dim=4096)
        P.op("dve", lambda: nc.vector.memset(ones1024, 1.0 / 1024.0), writes=[R_const])
        P.op("dve", lambda: nc.vector.memset(ones512, 1.0 / 512.0), writes=[R_const])
        ts("dve", d_hba, pcol("ba", 0, 4), 0.5, ALU.mult, [R_par], [R_der])
        ts("dve", d_hbx, pcol("bx", 0, 4), 0.5, ALU.mult, [R_par], [R_der])
        tt("dve", d_pbs, pcol("pool_b", 0, 4), pcol("pool_s", 0, 4), ALU.mult, [R_par], [R_der])
        act(d_tmp, pcol("lam", 0, 4), AF.Exp, [R_par], [R_der], scale=-1.0)
        act(d_tmp, d_tmp, AF.Ln, [R_der], [R_der], scale=1.0, bias=1.0)
        ts("dve", d_hc1, d_tmp, -4.0, ALU.mult, [R_der], [R_der])
        R_hcs = [Res(f"hc{i}") for i in range(4)]
        P.op("dve", lambda: nc.vector.memset(hc, 0.0), writes=R_hcs)
        P.op("dve", lambda: nc.vector.memset(pc, 0.5), writes=[R_pc])

        rstd_of = {}

        def pre(n):
            ncol = HALO if n < 0 else BN
            c0 = 0 if n < 0 else HALO + n * BN
            pss = PS[2][:, 0:ncol]
            for kt in range(8):
                xt, r_xt = xt_rot.next()
                P.dma("sp", f"d_xt{xt_rot.i}", xt[:, 0:ncol], xT[:, kt, c0:c0 + ncol], writes=[r_xt])
                sq, r_sq = sq_rot.next()
                act(sq[:, 0:ncol], xt[:, 0:ncol], AF.Square, [r_xt], [r_sq])
                mm(pss, ones1024, sq[:, 0:ncol], kt == 0, kt == 7, [r_sq, R_const], [PSR[2]], sig=True)
                ts("pool", xg[:, kt, 0:ncol], xt[:, 0:ncol], pcol("g_mix", kt), ALU.mult, [r_xt, R_par], [R_xg[kt]], s2=1.0, op1=ALU.mult)
            act(pss, pss, AF.Sqrt, [PSR[2]], [PSR[2]], bias=EPS)
            rs, r_rs = rstd_rot.next()
            P.op("dve", lambda: nc.vector.reciprocal(out=rs[:, 0:ncol], in_=pss), [PSR[2]], [r_rs])
            rstd_of[n] = (rs, r_rs)

        pu_i = [0]

        def inproj(q):
            ncol = HALO if q < 0 else BN
            dcol = 0 if q < 0 else HALO
            rs, r_rs = rstd_of[q]
            full = q >= PRE_BLK
            tiles = [0, 1, 2, 3] + ([8, 9, 10, 11, 4, 5, 6, 7] if full else ([8, 9, 10, 11] if q == PRE_BLK - 1 else []))
            E_of = {}
            for m in tiles:
                banks = [0, 1] if full else [0, 1, 7]
                b = banks[pu_i[0] % len(banks)]
                pu_i[0] += 1
                pool_halo = (m >= 8 and not full)
                cs = BN - HALO if pool_halo else 0
                nc_ = HALO if pool_halo else ncol
                pu = PS[b][:, 0:nc_]
                for kt in range(8):
                    mm(pu, w_in_sb[:, kt, m * 128:(m + 1) * 128], xg[:, kt, cs:cs + nc_], kt == 0, kt == 7,
                       [R_win[0 if m < 4 else (1 if m >= 8 else 2)], R_xg[kt]], [PSR[b]])
                if m < 4:
                    dst, rd = u_lru[:, m, dcol:dcol + ncol], R_ulru[m]
                elif m >= 8:
                    dd = 0 if pool_halo else dcol
                    dst, rd = u_pool[:, m - 8, dd:dd + nc_], R_upool[m - 8]
                else:
                    ev, rd = E_rot.next()
                    dst = ev
                    E_of[m - 4] = (ev, rd)
                    gate_E[m - 4] = (ev, rd)
                tt("dve", dst, pu, rs[:, cs:cs + nc_], ALU.mult, [PSR[b], r_rs], [rd])

        SQ_C = float(np.sqrt(0.044715))
        gate_E = {}

        def gate_block():
            for ct in range(4):
                ug, r_ug = gate_E[ct]
                act(F_views[ct], ug, AF.Square, [r_ug], [F_res[ct]], scale=SQ_C)
            for ct in range(4):
                ug, r_ug = gate_E[ct]
                stt(F_views[ct], F_views[ct], 1.0, ug, ALU.add, ALU.mult, [F_res[ct], r_ug], [F_res[ct]])
            for ct in range(4):
                act(F_views[ct], F_views[ct], AF.Tanh, [F_res[ct]], [F_res[ct]], scale=0.7978845608028654)
            for ct in range(4):
                ug, r_ug = gate_E[ct]
                stt(F_views[ct], F_views[ct], 1.0, ug, ALU.add, ALU.mult, [F_res[ct], r_ug], [F_res[ct]])

        lru_state = {}

        blk_state = {}

        def front_a(q):
            XC = [xc_rot.next() for _ in range(4)]
            for ct in range(4):
                act(XC[ct][0], u_lru[:, ct, HALO:HALO + BN], AF.Identity, [R_ulru[ct], R_par], [XC[ct][1]],
                    scale=pcol("conv_w", ct * 4 + 3), bias=pcol("conv_b", ct))
            for k in range(3):
                for ct in range(4):
                    xc, r_xc = XC[ct]
                    stt(xc, u_lru[:, ct, HALO - 3 + k:HALO - 3 + k + BN], pcol("conv_w", ct * 4 + k), xc, ALU.mult, ALU.add,
                        [R_ulru[ct], R_par, r_xc], [r_xc])
            blk_state[q] = {"XC": XC}

        def front_b(q):
            XC = blk_state[q]["XC"]
            XB = [xcb_rot.next() for _ in range(4)]
            for ct in range(4):
                act(XB[ct][0], XC[ct][0], AF.Copy, [XC[ct][1]], [XB[ct][1]])
            blk_state[q]["XB"] = XB

        def back(q):
            XC, XB = blk_state[q]["XC"], blk_state[q]["XB"]
            AA = [A_rot.next() for _ in range(4)]
            CC = [C_rot.next() for _ in range(4)]
            for pair in range(2):
                for ct in (2 * pair, 2 * pair + 1):
                    pr, pi = PS[3 + 2 * (ct % 2)], PS[4 + 2 * (ct % 2)]
                    r_pr, r_pi = PSR[3 + 2 * (ct % 2)], PSR[4 + 2 * (ct % 2)]
                    mm(pr[:, :], gates_sb[:, 0, ct, :], XB[ct][0], True, True, [R_gates, XB[ct][1]], [r_pr])
                    mm(pi[:, :], gates_sb[:, 1, ct, :], XB[ct][0], True, True, [R_gates, XB[ct][1]], [r_pi])
                for ct in (2 * pair, 2 * pair + 1):
                    pr, pi = PS[3 + 2 * (ct % 2)], PS[4 + 2 * (ct % 2)]
                    r_pr, r_pi = PSR[3 + 2 * (ct % 2)], PSR[4 + 2 * (ct % 2)]
                    act(AA[ct][0], pr[:, :], AF.Tanh, [r_pr, R_der], [AA[ct][1]], scale=0.5, bias=d_hba[:, ct:ct + 1])
                    act(CC[ct][0], pi[:, :], AF.Tanh, [r_pi, R_der], [CC[ct][1]], scale=0.5, bias=d_hbx[:, ct:ct + 1])
            for ct in range(4):
                a, r_a = AA[ct]
                act(a, a, AF.Exp, [r_a, R_der], [r_a], scale=d_hc1[:, ct:ct + 1], bias=d_hc1[:, ct:ct + 1])
            BB = [B_rot.next() for _ in range(4)]
            for ct in range(4):
                act(BB[ct][0], AA[ct][0], AF.Square, [AA[ct][1]], [BB[ct][1]])
            for ct in range(4):
                ts("pool", BB[ct][0], BB[ct][0], 1.0, ALU.min, [BB[ct][1]], [BB[ct][1]], s2=0.0, op1=ALU.max)
            for ct in range(4):
                xc, r_xc = XC[ct]
                stt(xc, CC[ct][0], 1.0, xc, ALU.add, ALU.mult, [CC[ct][1], r_xc], [r_xc])
            blk_state[q]["st"] = [(XC[ct][0], XC[ct][1], AA[ct][0], AA[ct][1], BB[ct][0], BB[ct][1]) for ct in range(4)]

        def lru_eblock(q):
            front_a(q)
            front_b(q)
            back(q)

        def lru_sblock(q):
            n = q - PRE_BLK
            lru_state = blk_state[q]["st"]
            for ct in range(4):
                xc, r_xc, a, r_a, bb, r_b = lru_state[ct]
                act(bb, bb, AF.Sqrt, [r_b], [r_b], scale=-1.0 / 16.0, bias=1.0 / 16.0)
            for ct in range(4):
                xc, r_xc, a, r_a, bb, r_b = lru_state[ct]
                tt("dve", xc, xc, bb, ALU.mult, [r_xc, r_b], [r_xc])
            for ct in range(4):
                xc, r_xc, a, r_a, bb, r_b = lru_state[ct]
                P.op("dve", lambda a=a, xc=xc, bb=bb, ct=ct: nc.vector.tensor_tensor_scan(
                    out=bb, data0=a, data1=xc, initial=hc[:, ct:ct + 1], op0=ALU.mult, op1=ALU.add),
                    [r_a, r_xc, R_hcs[ct]], [r_b])
            for ct in range(4):
                xc, r_xc, a, r_a, bb, r_b = lru_state[ct]
                if q < PRE_BLK:
                    ts("dve", hc[:, ct:ct + 1], bb[:, BN - 1:BN], pcol("pmask", q), ALU.mult, [r_b, R_par], [R_hcs[ct]])
                else:
                    P.op("dve", lambda bb=bb, ct=ct: nc.vector.tensor_copy(out=hc[:, ct:ct + 1], in_=bb[:, BN - 1:BN]), [r_b], [R_hcs[ct]])
            if q >= PRE_BLK:
                for ct in range(4):
                    xc, r_xc, a, r_a, bb, r_b = lru_state[ct]
                    tt("pool", V(RTILE[n][ct], F32, [512]), bb, F_views[ct], ALU.mult, [r_b, F_res[ct]], [RRES[n][ct]])

        def pool_epart(n):
            for gi in range(4):
                w = WINS[gi]
                U = u_pool[:, gi, :]
                rU = R_upool[gi]
                src, r_src = U, rU
                sh = 1
                lo = 1
                while sh < w:
                    s, r_s = S_rot.next()
                    tt("dve", s[:, lo:UW], src[:, lo:UW], src[:, lo - sh:UW - sh], ALU.add, [r_src], [r_s])
                    src, r_src = s, r_s
                    sh *= 2
                    lo = 2 * sh - 1 if sh < w else lo
                    lo = min(lo, HALO)
                pl, r_pl = pl_rot.next()
                stt(pl, src[:, HALO:UW], 1.0 / w, U[:, HALO:UW], ALU.mult, ALU.subtract, [r_src, rU], [r_pl])
                if n == 0:
                    ic = par[:, PC["invc"] + gi * 16:PC["invc"] + gi * 16 + 16]
                    s2, r_s2 = S_rot.next()
                    tt("dve", s2[:, 0:16], src[:, HALO:HALO + 16], ic, ALU.mult, [r_src, R_par], [r_s2])
                    tt("dve", pl[:, 0:16], s2[:, 0:16], U[:, HALO:HALO + 16], ALU.subtract, [r_s2, rU], [r_pl])
                mm(PS[7][:, :], poolw_sb[:, gi, :], pl, True, True, [R_poolw, r_pl], [PSR[7]])
                yp, r_yp = yp_views[gi], yp_res[gi]
                act(yp, PS[7][:, :], AF.Identity, [PSR[7], R_par, R_der], [r_yp], scale=pcol("pool_s", gi), bias=d_pbs[:, gi:gi + 1])
                ysq, r_ysq = ysq_rot.next()
                act(ysq, yp, AF.Square, [r_yp], [r_ysq])
                mm(PS[2][:, :], ones512, ysq, gi == 0, gi == 3, [r_ysq, R_const], [PSR[2]], sig=True)

        def pool_spart(n):
            act(PS[2][:, :], PS[2][:, :], AF.Sqrt, [PSR[2]], [PSR[2]], bias=EPS)
            rs, r_rs = rstd_rot.next()
            P.op("dve", lambda: nc.vector.reciprocal(out=rs, in_=PS[2][:, :]), [PSR[2]], [r_rs])
            ynp = V(RTILE[n][8], BF16, [4, 512])
            for gi in range(4):
                stt(ynp[:, gi, :], yp_views[gi], pcol("g_pool", gi), rs, ALU.mult, ALU.mult,
                    [yp_res[gi], R_par, r_rs], [RRES[n][8 + gi // 2]])

        def halo_copy(q):
            for ct in range(4):
                P.op("pool", lambda ct=ct: nc.gpsimd.tensor_copy(out=u_lru[:, ct, 0:HALO], in_=u_lru[:, ct, BN:BN + HALO]), [R_ulru[ct]], [R_ulru[ct]])
                if q >= PRE_BLK:
                    P.op("pool", lambda ct=ct: nc.gpsimd.tensor_copy(out=u_pool[:, ct, 0:HALO], in_=u_pool[:, ct, BN:BN + HALO]), [R_upool[ct]], [R_upool[ct]])

        pre(-1)
        inproj(-1)
        pre(0)
        inproj(0)
        front_a(0)
        front_b(0)
        halo_copy(0)
        pre(1)
        for q in range(PRE_BLK):
            if q + 1 < PRE_BLK:
                inproj(q + 1)
                front_a(q + 1)
            back(q)
            lru_sblock(q)
            if q + 1 < PRE_BLK:
                front_b(q + 1)
                halo_copy(q + 1)
            if q + 2 <= PRE_BLK:
                pre(q + 2)
        inproj(PRE_BLK)
        front_a(PRE_BLK)
        front_b(PRE_BLK)
        pre(PRE_BLK + 1)
        for q in range(PRE_BLK, NQ):
            n = q - PRE_BLK
            gate_block()
            pool_epart(n)
            halo_copy(q)
            if q + 1 < NQ:
                inproj(q + 1)
            back(q)
            if q + 1 < NQ:
                front_a(q + 1)
            lru_sblock(q)
            pool_spart(n)
            if q + 1 < NQ:
                front_b(q + 1)
            if q + 2 < NQ:
                pre(q + 2)

        def inherit(new_res, old_res):
            for rr in new_res:
                for s in old_res:
                    if s.w is not None:
                        k, v = s.w
                        if rr.r.get(k, 0) < v:
                            rr.r[k] = v
                    for k, v in s.r.items():
                        if rr.r.get(k, 0) < v:
                            rr.r[k] = v

        wout_res_half = [xc_rot.r[0:4], A_rot.r]
        for i in range(2):
            inherit([R_wout[i]], wout_res_half[i])
            P.dma("pool", f"d_wout{i}", w_out_sb[:, 4 * i:4 * i + 4, :], w_out_d[:, 4 * i:4 * i + 4, :],
                  writes=[R_wout[i]], max_dma_last_dim=4096)
        if STOP == 'exch':
            P.close(); return nc
        a1_stream_res = (xc_rot.r[4:8] + xt_rot.r + sq_rot.r + R_xg + R_ulru + R_upool + rstd_rot.r + B_rot.r)
        a1_t0_res = E_rot.r + C_rot.r + xcb_rot.r + S_rot.r + pl_rot.r + ysq_rot.r

        o = STR_OFF
        ynl_rot = Rot([V(o + i * 4 * KB, BF16, [4, 512]) for i in range(2)], "ynl"); o += 8 * KB
        x2_rot = Rot([V(o + i * 2 * KB, F32, [512]) for i in range(4)], "x2"); o += 8 * KB
        sq2_rot = Rot([V(o + i * KB, BF16, [512]) for i in range(2)], "sq2"); o += 2 * KB
        rstd2_rot = Rot([V(o + i * 2 * KB, F32, [512]) for i in range(2)], "rstd2"); o += 4 * KB
        assert o <= T0_OFF
        barrier_srcs = a1_stream_res
        for rr in ynl_rot.r + x2_rot.r + sq2_rot.r + rstd2_rot.r:
            for s in barrier_srcs:
                if s.w is not None:
                    k, v = s.w
                    if rr.r.get(k, 0) < v:
                        rr.r[k] = v
                for k, v in s.r.items():
                    if rr.r.get(k, 0) < v:
                        rr.r[k] = v

        WS = [W_IN_OFF, W_IN_OFF + 12 * KB, T0_OFF]
        w1s = [V(WS[i], BF16, [8, GW]) for i in range(3)]
        w3s = [V(WS[i] + 4 * KB, BF16, [8, GW]) for i in range(3)]
        w2s = [V(WS[i] + 8 * KB, BF16, [2, 1024]) for i in range(3)]
        R_w1 = [Res(f"w1s{i}") for i in range(3)]
        R_w3 = [Res(f"w3s{i}") for i in range(3)]
        R_w2 = [Res(f"w2s{i}") for i in range(3)]
        for RW in (R_w1, R_w3, R_w2):
            inherit(RW[0:2], R_win)
            inherit(RW[2:3], a1_t0_res)

        def load_group(g):
            s = g % 3
            P.dma("pool", f"d_w1_{s}", w1s[s], w1_d[g], writes=[R_w1[s]], max_dma_last_dim=4096)
            P.dma("pool", f"d_w3_{s}", w3s[s], w3_d[g], writes=[R_w3[s]], max_dma_last_dim=4096)
            P.dma("pool", f"d_w2_{s}", w2s[s], w2_d[g], writes=[R_w2[s]], max_dma_last_dim=4096)

        if STOP is None or not STOP.startswith('a2c'):
            load_group(0)
            load_group(1)
            load_group(2)
        if STOP == 'ldg':
            for k in ["d_wout0", "d_wout1"] + [f"d_w{t}_{s_}" for t in (1, 2, 3) for s_ in range(3)]:
                nc.sync.wait_ge(P.sems[k], P.cnt[k])
            P.close(); return nc

        po_banks = [0, 1, 3, 4]
        po_i = [0]

        class _StopA2(Exception):
            pass

        def a2(n):
            yv = [V(RTILE[n][ct], F32, [512]) for ct in range(4)]
            for ct in range(4):
                sq, r_sq = sq2_rot.next()
                act(sq, yv[ct], AF.Square, [RRES[n][ct]], [r_sq])
                mm(PS[2][:, :], ones512, sq, ct == 0, ct == 3, [r_sq, R_const], [PSR[2]], sig=True)
            act(PS[2][:, :], PS[2][:, :], AF.Sqrt, [PSR[2]], [PSR[2]], bias=EPS)
            rs, r_rs = rstd2_rot.next()
            P.op("dve", lambda: nc.vector.reciprocal(out=rs, in_=PS[2][:, :]), [PSR[2]], [r_rs])
            ynl, r_ynl = ynl_rot.next()
            for ct in range(4):
                stt(ynl[:, ct, :], yv[ct], pcol("g_lru", ct), rs, ALU.mult, ALU.mult, [RRES[n][ct], R_par, r_rs], [r_ynl])
            if STOP == 'a2c1':
                raise _StopA2()
            ynp = V(RTILE[n][8], BF16, [4, 512])
            hres = V(RTILE[n][0], F32, [8, 512])
            c0 = HALO + (PRE_BLK + n) * BN
            for m in range(8):
                b = po_banks[po_i[0] % 4]
                po_i[0] += 1
                for kt in range(8):
                    rhs = ynl[:, kt, :] if kt < 4 else ynp[:, kt - 4, :]
                    rr = r_ynl if kt < 4 else RRES[n][8 + (kt - 4) // 2]
                    mm(PS[b][:, :], w_out_sb[:, kt, m * 128:(m + 1) * 128], rhs, kt == 0, kt == 7, [R_wout[kt // 4], rr], [PSR[b]])
                x2, r_x2 = x2_rot.next()
                P.dma("sp", f"d_x2_{x2_rot.i}", x2, xT[:, m, c0:c0 + BN], writes=[r_x2])
                tt("dve", hres[:, m, :], PS[b][:, :], x2, ALU.add, [PSR[b], r_x2], [RRES[n][m]])
            if STOP == 'a2c2':
                raise _StopA2()
            for m in range(8):
                sq, r_sq = sq2_rot.next()
                act(sq, hres[:, m, :], AF.Square, [RRES[n][m]], [r_sq])
                mm(PS[2][:, :], ones1024, sq, m == 0, m == 7, [r_sq, R_const], [PSR[2]], sig=True)
            act(PS[2][:, :], PS[2][:, :], AF.Sqrt, [PSR[2]], [PSR[2]], bias=EPS)
            rs, r_rs = rstd2_rot.next()
            P.op("dve", lambda: nc.vector.reciprocal(out=rs, in_=PS[2][:, :]), [PSR[2]], [r_rs])
            hffn = V(RTILE[n][8], BF16, [8, 512])
            for m in range(8):
                stt(hffn[:, m, :], hres[:, m, :], pcol("g_ffn", m), rs, ALU.mult, ALU.mult,
                    [RRES[n][m], R_par, r_rs], [RRES[n][8 + m // 2]])

        try:
            for n in range(NB):
                a2(n)
                if STOP == 'a2c3':
                    raise _StopA2()
        except _StopA2:
            P.close(); return nc

        if STOP in ('a2', 'a2c'):
            P.close(); return nc
        ff = [V(W_OUT_OFF + i * 8 * KB, BF16, [2, 4, 512]) for i in range(2)]
        R_ff = [[[Res(f"ff{i}_{j}_{n}") for n in range(NB)] for j in range(2)] for i in range(2)]
        for i in range(2):
            for j in range(2):
                inherit(R_ff[i][j], R_wout)
        o = STR_OFF
        sl_rot = Rot([V(o + i * 2 * KB, F32, [512]) for i in range(3)], "sl"); o += 6 * KB
        ot_rot = Rot([V(o + i * 2 * KB, F32, [512]) for i in range(4)], "ot"); o += 8 * KB
        sq3_rot = Rot([V(o + i * KB, BF16, [512]) for i in range(2)], "sq3"); o += 2 * KB
        rstd3_rot = Rot([V(o + i * 2 * KB, F32, [512]) for i in range(2)], "rstd3"); o += 4 * KB
        a2_res = ynl_rot.r + x2_rot.r + sq2_rot.r + rstd2_rot.r
        inherit(sl_rot.r + ot_rot.r + sq3_rot.r + rstd3_rot.r, a2_res)

        pa_i = [0]
        pd_i = [0]
        out_sems = []

        def up(g, n, js=(0, 1)):
            s = g % 3
            hffn = V(RTILE[n][8], BF16, [8, 512])
            for j in js:
                ia = pa_i[0] % 2
                pa_i[0] += 1
                pa, r_pa = PS[0 + ia], PSR[0 + ia]
                pb, r_pb = PS[3 + ia], PSR[3 + ia]
                for kt in range(8):
                    mm(pa[:, :], w1s[s][:, kt, j * 128:(j + 1) * 128], hffn[:, kt, :], kt == 0, kt == 7,
                       [R_w1[s], RRES[n][8 + kt // 2]], [r_pa])
                for kt in range(8):
                    mm(pb[:, :], w3s[s][:, kt, j * 128:(j + 1) * 128], hffn[:, kt, :], kt == 0, kt == 7,
                       [R_w3[s], RRES[n][8 + kt // 2]], [r_pb])
                sl, r_sl = sl_rot.next()
                act(sl, pa[:, :], AF.Silu, [r_pa], [r_sl])
                tt("dve", ff[g % 2][:, j, n, :], pb[:, :], sl, ALU.mult, [r_pb, r_sl], [R_ff[g % 2][j][n]])

        def down(g, n, last, ms=range(8)):
            s = g % 3
            hres = V(RTILE[n][0], F32, [8, 512])
            for m in ms:
                ib = (5, 6, 7)[pd_i[0] % 3]
                pd_i[0] += 1
                for j in range(2):
                    mm(PS[ib][:, :], w2s[s][:, j, m * 128:(m + 1) * 128], ff[g % 2][:, j, n, :], j == 0, j == 1,
                       [R_w2[s], R_ff[g % 2][j][n]], [PSR[ib]])
                tt("dve", hres[:, m, :], PS[ib][:, :], hres[:, m, :], ALU.add, [PSR[ib], RRES[n][m]], [RRES[n][m]])
            if last and 7 in ms:
                final_norm(n)

        def final_norm(n):
            hres = V(RTILE[n][0], F32, [8, 512])
            for m in range(8):
                sq, r_sq = sq3_rot.next()
                act(sq, hres[:, m, :], AF.Square, [RRES[n][m]], [r_sq])
                mm(PS[2][:, :], ones1024, sq, m == 0, m == 7, [r_sq, R_const], [PSR[2]], sig=True)
            act(PS[2][:, :], PS[2][:, :], AF.Sqrt, [PSR[2]], [PSR[2]], bias=EPS)
            rs, r_rs = rstd3_rot.next()
            P.op("dve", lambda: nc.vector.reciprocal(out=rs, in_=PS[2][:, :]), [PSR[2]], [r_rs])
            for m in range(8):
                ot, r_ot = ot_rot.next()
                stt(ot, hres[:, m, :], pcol("g_fin", m), rs, ALU.mult, ALU.mult, [RRES[n][m], R_par, r_rs], [r_ot])
                key = f"d_out{ot_rot.i}"
                P.dma("sp", key, out_d[:, m, n * BN:(n + 1) * BN], ot, reads=[r_ot])
                if key not in out_sems:
                    out_sems.append(key)

        for g in range(NG):
            for n in range(NB):
                up(g, n, (0,))
                if g > 0:
                    down(g - 1, n, False, range(0, 4))
                up(g, n, (1,))
                if g > 0:
                    down(g - 1, n, False, range(4, 8))
            if g > 0 and g + 2 < NG:
                load_group(g + 2)
        for n in range(NB):
            down(NG - 1, n, True)

        for key in out_sems:
            nc.sync.wait_ge(P.sems[key], P.cnt[key])
        print(f"[kernel] instructions={P.n_ins} waits={P.n_wait}")
        P.close()
    return nc


def _host_layout(inp):
    f = lambda a: np.ascontiguousarray(np.asarray(a, dtype=np.float32))
    x = f(inp["x"])
    shared = {}
    shared["w_in"] = f(f(inp["w_in"])[0].reshape(8, 128, 1536).transpose(1, 0, 2))
    shared["w_out"] = f(f(inp["w_out"])[0].reshape(8, 128, 1024).transpose(1, 0, 2))
    shared["w1g"] = f(f(inp["ffn_w1"])[0].reshape(8, 128, NG, GW).transpose(2, 1, 0, 3))
    shared["w3g"] = f(f(inp["ffn_w3"])[0].reshape(8, 128, NG, GW).transpose(2, 1, 0, 3))
    shared["w2g"] = f(f(inp["ffn_w2"])[0].reshape(NG, 2, 128, 1024).transpose(0, 2, 1, 3))
    gates = np.zeros((128, 2, 4, 128), np.float32)
    for gi, nm in enumerate(["gate_a_w", "gate_x_w"]):
        w = f(inp[nm])[0]
        for h in range(8):
            ct, hh = divmod(h, 2)
            gates[hh * 64:(hh + 1) * 64, gi, ct, hh * 64:(hh + 1) * 64] = w[h]
    shared["gates"] = gates
    shared["poolw"] = f(f(inp["pool_w"])[0].transpose(1, 0, 2))
    par = np.zeros((128, NPAR), np.float32)
    col = lambda v, nt: f(v).reshape(nt, 128).T
    par[:, PC["g_mix"]:PC["g_mix"] + 8] = col(f(inp["norm_mix_g"])[0], 8)
    cw = f(inp["conv_w"])[0]
    for ct in range(4):
        for k in range(4):
            par[:, PC["conv_w"] + ct * 4 + k] = cw[k, ct * 128:(ct + 1) * 128]
    par[:, PC["conv_b"]:PC["conv_b"] + 4] = col(f(inp["conv_b"])[0], 4)
    par[:, PC["ba"]:PC["ba"] + 4] = col(f(inp["gate_a_b"])[0], 4)
    par[:, PC["bx"]:PC["bx"] + 4] = col(f(inp["gate_x_b"])[0], 4)
    par[:, PC["lam"]:PC["lam"] + 4] = col(f(inp["lru_lambda"])[0], 4)
    par[:, PC["pool_b"]:PC["pool_b"] + 4] = col(f(inp["pool_b"])[0], 4)
    par[:, PC["pool_s"]:PC["pool_s"] + 4] = col(f(inp["pool_scale"])[0], 4)
    par[:, PC["g_lru"]:PC["g_lru"] + 4] = col(f(inp["norm_lru_g"])[0], 4)
    par[:, PC["g_pool"]:PC["g_pool"] + 4] = col(f(inp["norm_pool_g"])[0], 4)
    par[:, PC["g_ffn"]:PC["g_ffn"] + 8] = col(f(inp["norm_ffn_g"])[0], 8)
    par[:, PC["g_fin"]:PC["g_fin"] + 8] = col(f(inp["final_norm_g"]), 8)
    in_maps = []
    for c in range(NCORES):
        b, k = divmod(c, 4)
        xs = np.zeros((XCOLS, D), np.float32)
        lo = k * T - PRE_BLK * BN - HALO
        src0 = max(lo, 0)
        xs[src0 - lo:] = x[b, src0:(k + 1) * T]
        xTc = f(xs.T.reshape(8, 128, XCOLS).transpose(1, 0, 2))
        p = par.copy()
        for gi, w in enumerate(WINS):
            for t in range(16):
                cnt = min(t + 1, w) if k == 0 else w
                p[:, PC["invc"] + gi * 16 + t] = 1.0 / cnt
        for q in range(PRE_BLK):
            p[:, PC["pmask"] + q] = 1.0 if (k * T - PRE_BLK * BN + q * BN) >= 0 else 0.0
        m = dict(shared)
        m["xT"] = xTc
        m["params"] = p
        in_maps.append(m)
    return in_maps


_NC_CACHE = {}


def kernel(**inputs):
    in_maps = _host_layout(inputs)
    if "nc" not in _NC_CACHE:
        _NC_CACHE["nc"] = build_nc()
    nc = _NC_CACHE["nc"]
    res = run_bass_kernel_spmd(nc, in_maps, core_ids=list(range(NCORES)))
    out = np.empty((2, 4 * T, D), np.float32)
    for c in range(NCORES):
        b, k = divmod(c, 4)
        o = np.asarray(res.results[c]["out"])
        out[b, k * T:(k + 1) * T, :] = o.transpose(2, 1, 0).reshape(T, D)
    return out
```

```python
import numpy as np
import concourse.bass as bass
import concourse.mybir as mybir
from concourse.bass_utils import run_bass_kernel_spmd

F32 = mybir.dt.float32
BF16 = mybir.dt.bfloat16
AF = mybir.ActivationFunctionType
ALU = mybir.AluOpType

NCORES = 8
D = 1024
T = 2048
HALO = 16
NB = 4
BN = 512
DFF = 2816
NG = 11
GW = 256
PRE_BLK = 12
NQ = PRE_BLK + NB
XCOLS = HALO + NQ * BN
EPS = 1e-6
WINS = (2, 4, 8, 16)

PC = {}
_o = 0
for _nm, _w in [("g_mix", 8), ("conv_w", 16), ("conv_b", 4), ("ba", 4), ("bx", 4), ("lam", 4), ("pool_b", 4),
                ("pool_s", 4), ("g_lru", 4), ("g_pool", 4), ("g_ffn", 8), ("g_fin", 8), ("invc", 64), ("pmask", 12)]:
    PC[_nm] = _o
    _o += _w
NPAR = _o

KB = 1024


class Res:
    __slots__ = ("name", "w", "r")

    def __init__(self, name):
        self.name = name
        self.w = None
        self.r = {}


class Prog:
    def __init__(self, nc):
        self.nc = nc
        self.eng = {"pe": nc.tensor, "act": nc.scalar, "dve": nc.vector, "pool": nc.gpsimd, "sp": nc.sync}
        self.sems = {}
        self.cnt = {}
        self.known = {e: {} for e in self.eng}
        self._stack = []
        for e in self.eng:
            self._mk(e)
        self.n_wait = 0
        self.n_ins = 0

    def _mk(self, key):
        cm = self.nc.semaphore("s_" + key)
        h = cm.__enter__()
        self._stack.append(cm)
        self.sems[key] = h
        self.cnt[key] = 0
        return h

    def close(self):
        for cm in reversed(self._stack):
            cm.__exit__(None, None, None)

    def _need(self, e, tok):
        if tok is None:
            return
        k, v = tok
        if k == "pe" and e == "pe":
            return
        if self.known[e].get(k, 0) < v:
            self.eng[e].wait_ge(self.sems[k], v)
            self.known[e][k] = v
            self.n_wait += 1

    def deps(self, e, reads, writes):
        for r in reads:
            self._need(e, r.w)
        for w in writes:
            self._need(e, w.w)
            for k, v in w.r.items():
                self._need(e, (k, v))

    def mark(self, tok, reads, writes):
        k, v = tok
        for r in reads:
            if r.r.get(k, 0) < v:
                r.r[k] = v
        for w in writes:
            w.w = tok
            w.r = {}

    def op(self, e, fn, reads=(), writes=(), signal=True):
        self.deps(e, reads, writes)
        ins = fn()
        self.n_ins += 1
        if signal:
            self.cnt[e] += 1
            ins.then_inc(self.sems[e], 1)
            tok = (e, self.cnt[e])
        else:
            tok = (e, self.cnt[e] + 1)
        self.mark(tok, reads, writes)
        return ins

    def dma(self, q, semkey, out, in_, reads=(), writes=(), **kw):
        if semkey not in self.sems:
            self._mk(semkey)
        self.deps(q, reads, writes)
        ins = self.eng[q].dma_start(out=out, in_=in_, **kw)
        self.cnt[semkey] += 16
        ins.then_inc(self.sems[semkey], 16)
        self.mark((semkey, self.cnt[semkey]), reads, writes)
        self.n_ins += 1
        return ins


class Rot:
    def __init__(self, views, name):
        self.v = views
        self.r = [Res(f"{name}{i}") for i in range(len(views))]
        self.i = -1

    def next(self):
        self.i = (self.i + 1) % len(self.v)
        return self.v[self.i], self.r[self.i]


STOP = None


def build_nc():
    nc = bass.Bass("TRN2", target_bir_lowering=False)
    dt_in = lambda name, shape: nc.dram_tensor(name, shape, F32, kind="ExternalInput").ap()
    xT = dt_in("xT", [128, 8, XCOLS])
    w_in_d = dt_in("w_in", [128, 8, 1536])
    w_out_d = dt_in("w_out", [128, 8, 1024])
    w1_d = dt_in("w1g", [NG, 128, 8, GW])
    w3_d = dt_in("w3g", [NG, 128, 8, GW])
    w2_d = dt_in("w2g", [NG, 128, 2, 1024])
    gates_d = dt_in("gates", [128, 2, 4, 128])
    poolw_d = dt_in("poolw", [128, 4, 128])
    par_d = dt_in("params", [128, NPAR])
    out_d = nc.dram_tensor("out", [128, 8, T], F32, kind="ExternalOutput").ap()

    ARENA = 204 * KB
    with (
        nc.sbuf_tensor("arena", [128, ARENA // 4], F32) as arena,
        nc.sbuf_tensor("misc", [128, 512], F32) as misc,
        nc.psum_tensor("ps0", [128, 512], F32) as ps0, nc.psum_tensor("ps1", [128, 512], F32) as ps1,
        nc.psum_tensor("ps2", [128, 512], F32) as ps2, nc.psum_tensor("ps3", [128, 512], F32) as ps3,
        nc.psum_tensor("ps4", [128, 512], F32) as ps4, nc.psum_tensor("ps5", [128, 512], F32) as ps5,
        nc.psum_tensor("ps6", [128, 512], F32) as ps6, nc.psum_tensor("ps7", [128, 512], F32) as ps7,
    ):
        P = Prog(nc)
        PS = [ps0, ps1, ps2, ps3, ps4, ps5, ps6, ps7]
        PSR = [Res(f"ps{i}") for i in range(8)]

        def V(off, dtype, shape):
            n = int(np.prod(shape))
            esz = 4 if dtype == F32 else 2
            nb = n * esz
            assert off % 4 == 0 and nb % 4 == 0 and off + nb <= ARENA, (off, nb)
            ap = arena[:, off // 4:(off + nb) // 4]
            if dtype != F32:
                ap = ap.bitcast(dtype)
            if len(shape) == 2:
                ap = ap.rearrange("p (a b) -> p a b", a=shape[0])
            elif len(shape) == 3:
                ap = ap.rearrange("p (a b c) -> p a b c", a=shape[0], b=shape[1])
            return ap

        R_OFF = 0
        RTILE = [[R_OFF + n * 24 * KB + i * 2 * KB for i in range(12)] for n in range(NB)]
        RRES = [[Res(f"R{n}_{i}") for i in range(12)] for n in range(NB)]
        W_IN_OFF = 96 * KB
        W_OUT_OFF = 120 * KB
        GATES_OFF = 136 * KB
        POOLW_OFF = 138 * KB
        ZERO_OFF = 139 * KB
        ONES_OFF = 141 * KB
        STR_OFF = 142 * KB
        T0_OFF = 185 * KB + 512

        w_in_sb = V(W_IN_OFF, BF16, [8, 1536])
        w_out_sb = V(W_OUT_OFF, BF16, [8, 1024])
        gates_sb = V(GATES_OFF, BF16, [2, 4, 128])
        poolw_sb = V(POOLW_OFF, BF16, [4, 128])
        zeros = V(ZERO_OFF, F32, [512])
        ones1024 = V(ONES_OFF, BF16, [128])
        ones512 = V(ONES_OFF + 256, BF16, [128])
        par = misc[:, 0:NPAR]
        DER = NPAR
        d_hba = misc[:, DER + 0:DER + 4]
        d_hbx = misc[:, DER + 4:DER + 8]
        d_hc1 = misc[:, DER + 8:DER + 12]
        d_pbs = misc[:, DER + 12:DER + 16]
        d_tmp = misc[:, DER + 16:DER + 20]
        hc = misc[:, DER + 20:DER + 24]
        pc = misc[:, DER + 24:DER + 28]
        ab = misc[:, DER + 28:DER + 36]
        hs = misc[:, DER + 36:DER + 40]
        hs_t = misc[:, DER + 40:DER + 44]
        gat = misc[:, DER + 48:DER + 48 + 64].rearrange("p (r f) -> p r f", r=8)
        assert DER + 48 + 64 <= 512
        R_par = Res("par"); R_der = Res("der"); R_hc = Res("hc"); R_pc = Res("pc"); R_ab = Res("ab")
        R_hs = Res("hs"); R_gat = Res("gat"); R_const = Res("const")
        R_win = [Res(f"win{i}") for i in range(4)]
        R_wout = [Res(f"wout{i}") for i in range(2)]
        R_gates = Res("gates"); R_poolw = Res("poolw")
        R_ccin = Res("ccin"); R_ccout = Res("ccout")

        def pcol(name, i=0, n=1):
            return par[:, PC[name] + i:PC[name] + i + n]

        o = STR_OFF
        xt_rot = Rot([V(RTILE[i // 2][6 + i % 2], F32, [512]) for i in range(8)], "xt")
        xt_rot.r = [RRES[i // 2][6 + i % 2] for i in range(8)]
        o += 4 * KB
        sq_rot = Rot([V(o + i * KB, BF16, [512]) for i in range(2)], "sq"); o += 2 * KB
        xg = V(o, BF16, [8, 512]); R_xg = [Res(f"xg{k}") for k in range(8)]; o += 8 * KB
        UW = HALO + BN
        u_lru = V(o, F32, [4, UW]); R_ulru = [Res(f"ulru{k}") for k in range(4)]; o += 4 * UW * 4
        u_pool = V(o, F32, [4, UW]); R_upool = [Res(f"upool{k}") for k in range(4)]; o += 4 * UW * 4
        rstd_rot = Rot([V(o + i * 2 * KB, F32, [512]) for i in range(2)], "rstd"); o += 4 * KB
        B_rot = Rot([V(o + i * 2 * KB, F32, [512]) for i in range(4)], "B"); o += 8 * KB
        assert o <= T0_OFF, (o, T0_OFF)
        o = T0_OFF
        E_rot = Rot([V(o + i * 2 * KB, F32, [512]) for i in range(4)], "E"); o += 8 * KB
        C_rot = Rot([V(RTILE[i][4], F32, [512]) for i in range(4)], "C")
        C_rot.r = [RRES[i][4] for i in range(4)]
        xcb_rot = Rot([V(RTILE[i][5], BF16, [512]) for i in range(4)], "xcb")
        xcb_rot.r = [RRES[i][5] for i in range(4)]
        SW = UW * 4
        S_rot = Rot([V(o + i * SW, F32, [UW]) for i in range(2)], "S"); o += 2 * SW
        pl_rot = Rot([V(o + i * KB, BF16, [512]) for i in range(2)], "pl"); o += 2 * KB
        ysq_rot = Rot([V(o + i * KB, BF16, [512]) for i in range(2)], "ysq"); o += 2 * KB
        XC7_OFF = o; o += 2 * KB
        assert o <= ARENA, (o, ARENA)
        xc_rot = Rot([V(W_OUT_OFF + i * 2 * KB, F32, [512]) for i in range(4)]
                     + [V(STR_OFF, F32, [512]), V(STR_OFF + 2 * KB, F32, [512]), V(ZERO_OFF, F32, [512]), V(XC7_OFF, F32, [512])], "xc")
        A_rot = Rot([V(W_OUT_OFF + 8 * KB + i * 2 * KB, F32, [512]) for i in range(4)], "A")
        F_views = [V(RTILE[n][10], F32, [512]) for n in range(NB)]
        F_res = [RRES[n][10] for n in range(NB)]
        yp_views = [V(RTILE[n][11], F32, [512]) for n in range(NB)]
        yp_res = [RRES[n][11] for n in range(NB)]

        def act(out, in_, func, reads, writes, scale=1.0, bias=None):
            kw = {} if bias is None else {"bias": bias}
            return P.op("act", lambda: nc.scalar.activation(out=out, in_=in_, func=func, scale=scale, **kw), reads, writes)

        def ts(e, out, in0, s1, op0, reads, writes, s2=None, op1=None):
            eng = nc.vector if e == "dve" else nc.gpsimd
            if op1 is None:
                return P.op(e, lambda: eng.tensor_scalar(out=out, in0=in0, scalar1=s1, scalar2=None, op0=op0), reads, writes)
            return P.op(e, lambda: eng.tensor_scalar(out=out, in0=in0, scalar1=s1, scalar2=s2, op0=op0, op1=op1), reads, writes)

        def tt(e, out, in0, in1, op, reads, writes):
            eng = nc.vector if e == "dve" else nc.gpsimd
            return P.op(e, lambda: eng.tensor_tensor(out=out, in0=in0, in1=in1, op=op), reads, writes)

        def stt(out, in0, scalar, in1, op0, op1, reads, writes):
            return P.op("dve", lambda: nc.vector.scalar_tensor_tensor(out=out, in0=in0, scalar=scalar, in1=in1, op0=op0, op1=op1), reads, writes)

        def mm(out, lhsT, rhs, start, stop, reads, writes, sig=False):
            return P.op("pe", lambda: nc.tensor.matmul(out, lhsT, rhs, start=start, stop=stop), reads, writes, signal=(stop or sig))

        P.dma("sp", "d_par", par, par_d, writes=[R_par])
        for i, c0_ in enumerate([0, 1024, 512]):
            P.dma("pool", f"d_win{i}", w_in_sb[:, :, c0_:c0_ + 512], w_in_d[:, :, c0_:c0_ + 512], writes=[R_win[i]], max_dma_last_dim=4096)
        P.dma("pool", "d_gates", gates_sb, gates_d, writes=[R_gates], max_dma_last_dim=4096)
        P.dma("pool", "d_poolw", poolw_sb, poolw_d, writes=[R_poolw], max_dma_last_dim=4096)
        P.op("dve", lambda: nc.vector.memset(ones1024, 1.0 / 1024.0), writes=[R_const])
        P.op("dve", lambda: nc.vector.memset(ones512, 1.0 / 512.0), writes=[R_const])
        ts("dve", d_hba, pcol("ba", 0, 4), 0.5, ALU.mult, [R_par], [R_der])
        ts("dve", d_hbx, pcol("bx", 0, 4), 0.5, ALU.mult, [R_par], [R_der])
        tt("dve", d_pbs, pcol("pool_b", 0, 4), pcol("pool_s", 0, 4), ALU.mult, [R_par], [R_der])
        act(d_tmp, pcol("lam", 0, 4), AF.Exp, [R_par], [R_der], scale=-1.0)
        act(d_tmp, d_tmp, AF.Ln, [R_der], [R_der], scale=1.0, bias=1.0)
        ts("dve", d_hc1, d_tmp, -4.0, ALU.mult, [R_der], [R_der])
        R_hcs = [Res(f"hc{i}") for i in range(4)]
        P.op("dve", lambda: nc.vector.memset(hc, 0.0), writes=R_hcs)
        P.op("dve", lambda: nc.vector.memset(pc, 0.5), writes=[R_pc])

        rstd_of = {}

        def pre(n):
            ncol = HALO if n < 0 else BN
            c0 = 0 if n < 0 else HALO + n * BN
            pss = PS[2][:, 0:ncol]
            for kt in range(8):
                xt, r_xt = xt_rot.next()
                P.dma("sp", f"d_xt{xt_rot.i}", xt[:, 0:ncol], xT[:, kt, c0:c0 + ncol], writes=[r_xt])
                sq, r_sq = sq_rot.next()
                act(sq[:, 0:ncol], xt[:, 0:ncol], AF.Square, [r_xt], [r_sq])
                mm(pss, ones1024, sq[:, 0:ncol], kt == 0, kt == 7, [r_sq, R_const], [PSR[2]], sig=True)
                ts("pool", xg[:, kt, 0:ncol], xt[:, 0:ncol], pcol("g_mix", kt), ALU.mult, [r_xt, R_par], [R_xg[kt]], s2=1.0, op1=ALU.mult)
            act(pss, pss, AF.Sqrt, [PSR[2]], [PSR[2]], bias=EPS)
            rs, r_rs = rstd_rot.next()
            P.op("dve", lambda: nc.vector.reciprocal(out=rs[:, 0:ncol], in_=pss), [PSR[2]], [r_rs])
            rstd_of[n] = (rs, r_rs)

        pu_i = [0]

        def inproj(q):
            ncol = HALO if q < 0 else BN
            dcol = 0 if q < 0 else HALO
            rs, r_rs = rstd_of[q]
            full = q >= PRE_BLK
            tiles = [0, 1, 2, 3] + ([8, 9, 10, 11, 4, 5, 6, 7] if full else ([8, 9, 10, 11] if q == PRE_BLK - 1 else []))
            E_of = {}
            for m in tiles:
                banks = [0, 1] if full else [0, 1, 7]
                b = banks[pu_i[0] % len(banks)]
                pu_i[0] += 1
                pool_halo = (m >= 8 and not full)
                cs = BN - HALO if pool_halo else 0
                nc_ = HALO if pool_halo else ncol
                pu = PS[b][:, 0:nc_]
                for kt in range(8):
                    mm(pu, w_in_sb[:, kt, m * 128:(m + 1) * 128], xg[:, kt, cs:cs + nc_], kt == 0, kt == 7,
                       [R_win[0 if m < 4 else (1 if m >= 8 else 2)], R_xg[kt]], [PSR[b]])
                if m < 4:
                    dst, rd = u_lru[:, m, dcol:dcol + ncol], R_ulru[m]
                elif m >= 8:
                    dd = 0 if pool_halo else dcol
                    dst, rd = u_pool[:, m - 8, dd:dd + nc_], R_upool[m - 8]
                else:
                    ev, rd = E_rot.next()
                    dst = ev
                    E_of[m - 4] = (ev, rd)
                    gate_E[m - 4] = (ev, rd)
                tt("dve", dst, pu, rs[:, cs:cs + nc_], ALU.mult, [PSR[b], r_rs], [rd])

        SQ_C = float(np.sqrt(0.044715))
        gate_E = {}

        def gate_block():
            for ct in range(4):
                ug, r_ug = gate_E[ct]
                act(F_views[ct], ug, AF.Square, [r_ug], [F_res[ct]], scale=SQ_C)
            for ct in range(4):
                ug, r_ug = gate_E[ct]
                stt(F_views[ct], F_views[ct], 1.0, ug, ALU.add, ALU.mult, [F_res[ct], r_ug], [F_res[ct]])
            for ct in range(4):
                act(F_views[ct], F_views[ct], AF.Tanh, [F_res[ct]], [F_res[ct]], scale=0.7978845608028654)
            for ct in range(4):
                ug, r_ug = gate_E[ct]
                stt(F_views[ct], F_views[ct], 1.0, ug, ALU.add, ALU.mult, [F_res[ct], r_ug], [F_res[ct]])

        lru_state = {}

        blk_state = {}

        def front_a(q):
            XC = [xc_rot.next() for _ in range(4)]
            for ct in range(4):
                act(XC[ct][0], u_lru[:, ct, HALO:HALO + BN], AF.Identity, [R_ulru[ct], R_par], [XC[ct][1]],
                    scale=pcol("conv_w", ct * 4 + 3), bias=pcol("conv_b", ct))
            for k in range(3):
                for ct in range(4):
                    xc, r_xc = XC[ct]
                    stt(xc, u_lru[:, ct, HALO - 3 + k:HALO - 3 + k + BN], pcol("conv_w", ct * 4 + k), xc, ALU.mult, ALU.add,
                        [R_ulru[ct], R_par, r_xc], [r_xc])
            blk_state[q] = {"XC": XC}

        def front_b(q):
            XC = blk_state[q]["XC"]
            XB = [xcb_rot.next() for _ in range(4)]
            for ct in range(4):
                act(XB[ct][0], XC[ct][0], AF.Copy, [XC[ct][1]], [XB[ct][1]])
            blk_state[q]["XB"] = XB

        def back(q):
            XC, XB = blk_state[q]["XC"], blk_state[q]["XB"]
            AA = [A_rot.next() for _ in range(4)]
            CC = [C_rot.next() for _ in range(4)]
            for pair in range(2):
                for ct in (2 * pair, 2 * pair + 1):
                    pr, pi = PS[3 + 2 * (ct % 2)], PS[4 + 2 * (ct % 2)]
                    r_pr, r_pi = PSR[3 + 2 * (ct % 2)], PSR[4 + 2 * (ct % 2)]
                    mm(pr[:, :], gates_sb[:, 0, ct, :], XB[ct][0], True, True, [R_gates, XB[ct][1]], [r_pr])
                    mm(pi[:, :], gates_sb[:, 1, ct, :], XB[ct][0], True, True, [R_gates, XB[ct][1]], [r_pi])
                for ct in (2 * pair, 2 * pair + 1):
                    pr, pi = PS[3 + 2 * (ct % 2)], PS[4 + 2 * (ct % 2)]
                    r_pr, r_pi = PSR[3 + 2 * (ct % 2)], PSR[4 + 2 * (ct % 2)]
                    act(AA[ct][0], pr[:, :], AF.Tanh, [r_pr, R_der], [AA[ct][1]], scale=0.5, bias=d_hba[:, ct:ct + 1])
                    act(CC[ct][0], pi[:, :], AF.Tanh, [r_pi, R_der], [CC[ct][1]], scale=0.5, bias=d_hbx[:, ct:ct + 1])
            for ct in range(4):
                a, r_a = AA[ct]
                act(a, a, AF.Exp, [r_a, R_der], [r_a], scale=d_hc1[:, ct:ct + 1], bias=d_hc1[:, ct:ct + 1])
            BB = [B_rot.next() for _ in range(4)]
            for ct in range(4):
                act(BB[ct][0], AA[ct][0], AF.Square, [AA[ct][1]], [BB[ct][1]])
            for ct in range(4):
                ts("pool", BB[ct][0], BB[ct][0], 1.0, ALU.min, [BB[ct][1]], [BB[ct][1]], s2=0.0, op1=ALU.max)
            for ct in range(4):
                xc, r_xc = XC[ct]
                stt(xc, CC[ct][0], 1.0, xc, ALU.add, ALU.mult, [CC[ct][1], r_xc], [r_xc])
            blk_state[q]["st"] = [(XC[ct][0], XC[ct][1], AA[ct][0], AA[ct][1], BB[ct][0], BB[ct][1]) for ct in range(4)]

        def lru_eblock(q):
            front_a(q)
            front_b(q)
            back(q)

        def lru_sblock(q):
            n = q - PRE_BLK
            lru_state = blk_state[q]["st"]
            for ct in range(4):
                xc, r_xc, a, r_a, bb, r_b = lru_state[ct]
                act(bb, bb, AF.Sqrt, [r_b], [r_b], scale=-1.0 / 16.0, bias=1.0 / 16.0)
            for ct in range(4):
                xc, r_xc, a, r_a, bb, r_b = lru_state[ct]
                tt("dve", xc, xc, bb, ALU.mult, [r_xc, r_b], [r_xc])
            for ct in range(4):
                xc, r_xc, a, r_a, bb, r_b = lru_state[ct]
                P.op("dve", lambda a=a, xc=xc, bb=bb, ct=ct: nc.vector.tensor_tensor_scan(
                    out=bb, data0=a, data1=xc, initial=hc[:, ct:ct + 1], op0=ALU.mult, op1=ALU.add),
                    [r_a, r_xc, R_hcs[ct]], [r_b])
            for ct in range(4):
                xc, r_xc, a, r_a, bb, r_b = lru_state[ct]
                if q < PRE_BLK:
                    ts("dve", hc[:, ct:ct + 1], bb[:, BN - 1:BN], pcol("pmask", q), ALU.mult, [r_b, R_par], [R_hcs[ct]])
                else:
                    P.op("dve", lambda bb=bb, ct=ct: nc.vector.tensor_copy(out=hc[:, ct:ct + 1], in_=bb[:, BN - 1:BN]), [r_b], [R_hcs[ct]])
            if q >= PRE_BLK:
                for ct in range(4):
                    xc, r_xc, a, r_a, bb, r_b = lru_state[ct]
                    tt("pool", V(RTILE[n][ct], F32, [512]), bb, F_views[ct], ALU.mult, [r_b, F_res[ct]], [RRES[n][ct]])

        def pool_epart(n):
            for gi in range(4):
                w = WINS[gi]
                U = u_pool[:, gi, :]
                rU = R_upool[gi]
                src, r_src = U, rU
                sh = 1
                lo = 1
                while sh < w:
                    s, r_s = S_rot.next()
                    tt("dve", s[:, lo:UW], src[:, lo:UW], src[:, lo - sh:UW - sh], ALU.add, [r_src], [r_s])
                    src, r_src = s, r_s
                    sh *= 2
                    lo = 2 * sh - 1 if sh < w else lo
                    lo = min(lo, HALO)
                pl, r_pl = pl_rot.next()
                stt(pl, src[:, HALO:UW], 1.0 / w, U[:, HALO:UW], ALU.mult, ALU.subtract, [r_src, rU], [r_pl])
                if n == 0:
                    ic = par[:, PC["invc"] + gi * 16:PC["invc"] + gi * 16 + 16]
                    s2, r_s2 = S_rot.next()
                    tt("dve", s2[:, 0:16], src[:, HALO:HALO + 16], ic, ALU.mult, [r_src, R_par], [r_s2])
                    tt("dve", pl[:, 0:16], s2[:, 0:16], U[:, HALO:HALO + 16], ALU.subtract, [r_s2, rU], [r_pl])
                mm(PS[7][:, :], poolw_sb[:, gi, :], pl, True, True, [R_poolw, r_pl], [PSR[7]])
                yp, r_yp = yp_views[gi], yp_res[gi]
                act(yp, PS[7][:, :], AF.Identity, [PSR[7], R_par, R_der], [r_yp], scale=pcol("pool_s", gi), bias=d_pbs[:, gi:gi + 1])
                ysq, r_ysq = ysq_rot.next()
                act(ysq, yp, AF.Square, [r_yp], [r_ysq])
                mm(PS[2][:, :], ones512, ysq, gi == 0, gi == 3, [r_ysq, R_const], [PSR[2]], sig=True)

        def pool_spart(n):
            act(PS[2][:, :], PS[2][:, :], AF.Sqrt, [PSR[2]], [PSR[2]], bias=EPS)
            rs, r_rs = rstd_rot.next()
            P.op("dve", lambda: nc.vector.reciprocal(out=rs, in_=PS[2][:, :]), [PSR[2]], [r_rs])
            ynp = V(RTILE[n][8], BF16, [4, 512])
            for gi in range(4):
                stt(ynp[:, gi, :], yp_views[gi], pcol("g_pool", gi), rs, ALU.mult, ALU.mult,
                    [yp_res[gi], R_par, r_rs], [RRES[n][8 + gi // 2]])

        def halo_copy(q):
            for ct in range(4):
                P.op("pool", lambda ct=ct: nc.gpsimd.tensor_copy(out=u_lru[:, ct, 0:HALO], in_=u_lru[:, ct, BN:BN + HALO]), [R_ulru[ct]], [R_ulru[ct]])
                if q >= PRE_BLK:
                    P.op("pool", lambda ct=ct: nc.gpsimd.tensor_copy(out=u_pool[:, ct, 0:HALO], in_=u_pool[:, ct, BN:BN + HALO]), [R_upool[ct]], [R_upool[ct]])

        pre(-1)
        inproj(-1)
        pre(0)
        inproj(0)
        front_a(0)
        front_b(0)
        halo_copy(0)
        pre(1)
        for q in range(PRE_BLK):
            if q + 1 < PRE_BLK:
                inproj(q + 1)
                front_a(q + 1)
            back(q)
            lru_sblock(q)
            if q + 1 < PRE_BLK:
                front_b(q + 1)
                halo_copy(q + 1)
            if q + 2 <= PRE_BLK:
                pre(q + 2)
        inproj(PRE_BLK)
        front_a(PRE_BLK)
        front_b(PRE_BLK)
        pre(PRE_BLK + 1)
        for q in range(PRE_BLK, NQ):
            n = q - PRE_BLK
            gate_block()
            pool_epart(n)
            halo_copy(q)
            if q + 1 < NQ:
                inproj(q + 1)
                front_a(q + 1)
            back(q)
            lru_sblock(q)
            pool_spart(n)
            if q + 1 < NQ:
                front_b(q + 1)
            if q + 2 < NQ:
                pre(q + 2)

        def inherit(new_res, old_res):
            for rr in new_res:
                for s in old_res:
                    if s.w is not None:
                        k, v = s.w
                        if rr.r.get(k, 0) < v:
                            rr.r[k] = v
                    for k, v in s.r.items():
                        if rr.r.get(k, 0) < v:
                            rr.r[k] = v

        wout_res_half = [xc_rot.r[0:4], A_rot.r]
        for i in range(2):
            inherit([R_wout[i]], wout_res_half[i])
            P.dma("pool", f"d_wout{i}", w_out_sb[:, 4 * i:4 * i + 4, :], w_out_d[:, 4 * i:4 * i + 4, :],
                  writes=[R_wout[i]], max_dma_last_dim=4096)
        if STOP == 'exch':
            P.close(); return nc
        a1_stream_res = (xc_rot.r[4:8] + xt_rot.r + sq_rot.r + R_xg + R_ulru + R_upool + rstd_rot.r + B_rot.r)
        a1_t0_res = E_rot.r + C_rot.r + xcb_rot.r + S_rot.r + pl_rot.r + ysq_rot.r

        o = STR_OFF
        ynl_rot = Rot([V(o + i * 4 * KB, BF16, [4, 512]) for i in range(2)], "ynl"); o += 8 * KB
        x2_rot = Rot([V(o + i * 2 * KB, F32, [512]) for i in range(4)], "x2"); o += 8 * KB
        sq2_rot = Rot([V(o + i * KB, BF16, [512]) for i in range(2)], "sq2"); o += 2 * KB
        rstd2_rot = Rot([V(o + i * 2 * KB, F32, [512]) for i in range(2)], "rstd2"); o += 4 * KB
        assert o <= T0_OFF
        barrier_srcs = a1_stream_res
        for rr in ynl_rot.r + x2_rot.r + sq2_rot.r + rstd2_rot.r:
            for s in barrier_srcs:
                if s.w is not None:
                    k, v = s.w
                    if rr.r.get(k, 0) < v:
                        rr.r[k] = v
                for k, v in s.r.items():
                    if rr.r.get(k, 0) < v:
                        rr.r[k] = v

        WS = [W_IN_OFF, W_IN_OFF + 12 * KB, T0_OFF]
        w1s = [V(WS[i], BF16, [8, GW]) for i in range(3)]
        w3s = [V(WS[i] + 4 * KB, BF16, [8, GW]) for i in range(3)]
        w2s = [V(WS[i] + 8 * KB, BF16, [2, 1024]) for i in range(3)]
        R_w1 = [Res(f"w1s{i}") for i in range(3)]
        R_w3 = [Res(f"w3s{i}") for i in range(3)]
        R_w2 = [Res(f"w2s{i}") for i in range(3)]
        for RW in (R_w1, R_w3, R_w2):
            inherit(RW[0:2], R_win)
            inherit(RW[2:3], a1_t0_res)

        def load_group(g):
            s = g % 3
            P.dma("pool", f"d_w1_{s}", w1s[s], w1_d[g], writes=[R_w1[s]], max_dma_last_dim=4096)
            P.dma("pool", f"d_w3_{s}", w3s[s], w3_d[g], writes=[R_w3[s]], max_dma_last_dim=4096)
            P.dma("pool", f"d_w2_{s}", w2s[s], w2_d[g], writes=[R_w2[s]], max_dma_last_dim=4096)

        if STOP is None or not STOP.startswith('a2c'):
            load_group(0)
            load_group(1)
            load_group(2)
        if STOP == 'ldg':
            for k in ["d_wout0", "d_wout1"] + [f"d_w{t}_{s_}" for t in (1, 2, 3) for s_ in range(3)]:
                nc.sync.wait_ge(P.sems[k], P.cnt[k])
            P.close(); return nc

        po_banks = [0, 1, 3, 4]
        po_i = [0]

        class _StopA2(Exception):
            pass

        a2_state = {}

        def a2_p1(n):
            yv = [V(RTILE[n][ct], F32, [512]) for ct in range(4)]
            for ct in range(4):
                sq, r_sq = sq2_rot.next()
                act(sq, yv[ct], AF.Square, [RRES[n][ct]], [r_sq])
                mm(PS[2][:, :], ones512, sq, ct == 0, ct == 3, [r_sq, R_const], [PSR[2]], sig=True)
            act(PS[2][:, :], PS[2][:, :], AF.Sqrt, [PSR[2]], [PSR[2]], bias=EPS)
            rs, r_rs = rstd2_rot.next()
            P.op("dve", lambda: nc.vector.reciprocal(out=rs, in_=PS[2][:, :]), [PSR[2]], [r_rs])
            ynl, r_ynl = ynl_rot.next()
            for ct in range(4):
                stt(ynl[:, ct, :], yv[ct], pcol("g_lru", ct), rs, ALU.mult, ALU.mult, [RRES[n][ct], R_par, r_rs], [r_ynl])
            a2_state[n] = (ynl, r_ynl)

        def a2_p2(n):
            ynl, r_ynl = a2_state[n]
            ynp = V(RTILE[n][8], BF16, [4, 512])
            hres = V(RTILE[n][0], F32, [8, 512])
            c0 = HALO + (PRE_BLK + n) * BN
            for m in range(8):
                b = po_banks[po_i[0] % 4]
                po_i[0] += 1
                for kt in range(8):
                    rhs = ynl[:, kt, :] if kt < 4 else ynp[:, kt - 4, :]
                    rr = r_ynl if kt < 4 else RRES[n][8 + (kt - 4) // 2]
                    mm(PS[b][:, :], w_out_sb[:, kt, m * 128:(m + 1) * 128], rhs, kt == 0, kt == 7, [R_wout[kt // 4], rr], [PSR[b]])
                x2, r_x2 = x2_rot.next()
                P.dma("sp", f"d_x2_{x2_rot.i}", x2, xT[:, m, c0:c0 + BN], writes=[r_x2])
                tt("dve", hres[:, m, :], PS[b][:, :], x2, ALU.add, [PSR[b], r_x2], [RRES[n][m]])

        def a2_p3(n):
            hres = V(RTILE[n][0], F32, [8, 512])
            for m in range(8):
                sq, r_sq = sq2_rot.next()
                act(sq, hres[:, m, :], AF.Square, [RRES[n][m]], [r_sq])
                mm(PS[2][:, :], ones1024, sq, m == 0, m == 7, [r_sq, R_const], [PSR[2]], sig=True)
            act(PS[2][:, :], PS[2][:, :], AF.Sqrt, [PSR[2]], [PSR[2]], bias=EPS)
            rs, r_rs = rstd2_rot.next()
            P.op("dve", lambda: nc.vector.reciprocal(out=rs, in_=PS[2][:, :]), [PSR[2]], [r_rs])
            hffn = V(RTILE[n][8], BF16, [8, 512])
            for m in range(8):
                stt(hffn[:, m, :], hres[:, m, :], pcol("g_ffn", m), rs, ALU.mult, ALU.mult,
                    [RRES[n][m], R_par, r_rs], [RRES[n][8 + m // 2]])

        a2_p1(0)
        for n in range(NB):
            a2_p2(n)
            if n + 1 < NB:
                a2_p1(n + 1)
            a2_p3(n)

        if STOP in ('a2', 'a2c'):
            P.close(); return nc
        ff = [V(W_OUT_OFF + i * 8 * KB, BF16, [2, 4, 512]) for i in range(2)]
        R_ff = [[[Res(f"ff{i}_{j}_{n}") for n in range(NB)] for j in range(2)] for i in range(2)]
        for i in range(2):
            for j in range(2):
                inherit(R_ff[i][j], R_wout)
        o = STR_OFF
        sl_rot = Rot([V(o + i * 2 * KB, F32, [512]) for i in range(3)], "sl"); o += 6 * KB
        ot_rot = Rot([V(o + i * 2 * KB, F32, [512]) for i in range(4)], "ot"); o += 8 * KB
        sq3_rot = Rot([V(o + i * KB, BF16, [512]) for i in range(2)], "sq3"); o += 2 * KB
        rstd3_rot = Rot([V(o + i * 2 * KB, F32, [512]) for i in range(2)], "rstd3"); o += 4 * KB
        a2_res = ynl_rot.r + x2_rot.r + sq2_rot.r + rstd2_rot.r
        inherit(sl_rot.r + ot_rot.r + sq3_rot.r + rstd3_rot.r, a2_res)

        pa_i = [0]
        pd_i = [0]
        out_sems = []

        def up(g, n, js=(0, 1)):
            s = g % 3
            hffn = V(RTILE[n][8], BF16, [8, 512])
            for j in js:
                ia = pa_i[0] % 2
                pa_i[0] += 1
                pa, r_pa = PS[0 + ia], PSR[0 + ia]
                pb, r_pb = PS[3 + ia], PSR[3 + ia]
                for kt in range(8):
                    mm(pa[:, :], w1s[s][:, kt, j * 128:(j + 1) * 128], hffn[:, kt, :], kt == 0, kt == 7,
                       [R_w1[s], RRES[n][8 + kt // 2]], [r_pa])
                for kt in range(8):
                    mm(pb[:, :], w3s[s][:, kt, j * 128:(j + 1) * 128], hffn[:, kt, :], kt == 0, kt == 7,
                       [R_w3[s], RRES[n][8 + kt // 2]], [r_pb])
                sl, r_sl = sl_rot.next()
                act(sl, pa[:, :], AF.Silu, [r_pa], [r_sl])
                tt("dve", ff[g % 2][:, j, n, :], pb[:, :], sl, ALU.mult, [r_pb, r_sl], [R_ff[g % 2][j][n]])

        def down(g, n, last, ms=range(8)):
            s = g % 3
            hres = V(RTILE[n][0], F32, [8, 512])
            for m in ms:
                ib = (5, 6, 7)[pd_i[0] % 3]
                pd_i[0] += 1
                for j in range(2):
                    mm(PS[ib][:, :], w2s[s][:, j, m * 128:(m + 1) * 128], ff[g % 2][:, j, n, :], j == 0, j == 1,
                       [R_w2[s], R_ff[g % 2][j][n]], [PSR[ib]])
                tt("dve", hres[:, m, :], PS[ib][:, :], hres[:, m, :], ALU.add, [PSR[ib], RRES[n][m]], [RRES[n][m]])
            if last and 7 in ms:
                final_norm(n)

        def final_norm(n):
            hres = V(RTILE[n][0], F32, [8, 512])
            for m in range(8):
                sq, r_sq = sq3_rot.next()
                act(sq, hres[:, m, :], AF.Square, [RRES[n][m]], [r_sq])
                mm(PS[2][:, :], ones1024, sq, m == 0, m == 7, [r_sq, R_const], [PSR[2]], sig=True)
            act(PS[2][:, :], PS[2][:, :], AF.Sqrt, [PSR[2]], [PSR[2]], bias=EPS)
            rs, r_rs = rstd3_rot.next()
            P.op("dve", lambda: nc.vector.reciprocal(out=rs, in_=PS[2][:, :]), [PSR[2]], [r_rs])
            for m in range(8):
                ot, r_ot = ot_rot.next()
                stt(ot, hres[:, m, :], pcol("g_fin", m), rs, ALU.mult, ALU.mult, [RRES[n][m], R_par, r_rs], [r_ot])
                key = f"d_out{ot_rot.i}"
                P.dma("sp", key, out_d[:, m, n * BN:(n + 1) * BN], ot, reads=[r_ot])
                if key not in out_sems:
                    out_sems.append(key)

        for g in range(NG):
            for n in range(NB):
                up(g, n, (0,))
                if g > 0:
                    down(g - 1, n, False, range(0, 4))
                up(g, n, (1,))
                if g > 0:
                    down(g - 1, n, False, range(4, 8))
            if g > 0 and g + 2 < NG:
                load_group(g + 2)
        for n in range(NB):
            down(NG - 1, n, True)

        for key in out_sems:
            nc.sync.wait_ge(P.sems[key], P.cnt[key])
        print(f"[kernel] instructions={P.n_ins} waits={P.n_wait}")
        P.close()
    return nc


def _host_layout(inp):
    f = lambda a: np.ascontiguousarray(np.asarray(a, dtype=np.float32))
    x = f(inp["x"])
    shared = {}
    shared["w_in"] = f(f(inp["w_in"])[0].reshape(8, 128, 1536).transpose(1, 0, 2))
    shared["w_out"] = f(f(inp["w_out"])[0].reshape(8, 128, 1024).transpose(1, 0, 2))
    shared["w1g"] = f(f(inp["ffn_w1"])[0].reshape(8, 128, NG, GW).transpose(2, 1, 0, 3))
    shared["w3g"] = f(f(inp["ffn_w3"])[0].reshape(8, 128, NG, GW).transpose(2, 1, 0, 3))
    shared["w2g"] = f(f(inp["ffn_w2"])[0].reshape(NG, 2, 128, 1024).transpose(0, 2, 1, 3))
    gates = np.zeros((128, 2, 4, 128), np.float32)
    for gi, nm in enumerate(["gate_a_w", "gate_x_w"]):
        w = f(inp[nm])[0]
        for h in range(8):
            ct, hh = divmod(h, 2)
            gates[hh * 64:(hh + 1) * 64, gi, ct, hh * 64:(hh + 1) * 64] = w[h]
    shared["gates"] = gates
    shared["poolw"] = f(f(inp["pool_w"])[0].transpose(1, 0, 2))
    par = np.zeros((128, NPAR), np.float32)
    col = lambda v, nt: f(v).reshape(nt, 128).T
    par[:, PC["g_mix"]:PC["g_mix"] + 8] = col(f(inp["norm_mix_g"])[0], 8)
    cw = f(inp["conv_w"])[0]
    for ct in range(4):
        for k in range(4):
            par[:, PC["conv_w"] + ct * 4 + k] = cw[k, ct * 128:(ct + 1) * 128]
    par[:, PC["conv_b"]:PC["conv_b"] + 4] = col(f(inp["conv_b"])[0], 4)
    par[:, PC["ba"]:PC["ba"] + 4] = col(f(inp["gate_a_b"])[0], 4)
    par[:, PC["bx"]:PC["bx"] + 4] = col(f(inp["gate_x_b"])[0], 4)
    par[:, PC["lam"]:PC["lam"] + 4] = col(f(inp["lru_lambda"])[0], 4)
    par[:, PC["pool_b"]:PC["pool_b"] + 4] = col(f(inp["pool_b"])[0], 4)
    par[:, PC["pool_s"]:PC["pool_s"] + 4] = col(f(inp["pool_scale"])[0], 4)
    par[:, PC["g_lru"]:PC["g_lru"] + 4] = col(f(inp["norm_lru_g"])[0], 4)
    par[:, PC["g_pool"]:PC["g_pool"] + 4] = col(f(inp["norm_pool_g"])[0], 4)
    par[:, PC["g_ffn"]:PC["g_ffn"] + 8] = col(f(inp["norm_ffn_g"])[0], 8)
    par[:, PC["g_fin"]:PC["g_fin"] + 8] = col(f(inp["final_norm_g"]), 8)
    in_maps = []
    for c in range(NCORES):
        b, k = divmod(c, 4)
        xs = np.zeros((XCOLS, D), np.float32)
        lo = k * T - PRE_BLK * BN - HALO
        src0 = max(lo, 0)
        xs[src0 - lo:] = x[b, src0:(k + 1) * T]
        xTc = f(xs.T.reshape(8, 128, XCOLS).transpose(1, 0, 2))
        p = par.copy()
        for gi, w in enumerate(WINS):
            for t in range(16):
                cnt = min(t + 1, w) if k == 0 else w
                p[:, PC["invc"] + gi * 16 + t] = 1.0 / cnt
        for q in range(PRE_BLK):
            p[:, PC["pmask"] + q] = 1.0 if (k * T - PRE_BLK * BN + q * BN) >= 0 else 0.0
        m = dict(shared)
        m["xT"] = xTc
        m["params"] = p
        in_maps.append(m)
    return in_maps


_NC_CACHE = {}


def kernel(**inputs):
    in_maps = _host_layout(inputs)
    if "nc" not in _NC_CACHE:
        _NC_CACHE["nc"] = build_nc()
    nc = _NC_CACHE["nc"]
    res = run_bass_kernel_spmd(nc, in_maps, core_ids=list(range(NCORES)))
    out = np.empty((2, 4 * T, D), np.float32)
    for c in range(NCORES):
        b, k = divmod(c, 4)
        o = np.asarray(res.results[c]["out"])
        out[b, k * T:(k + 1) * T, :] = o.transpose(2, 1, 0).reshape(T, D)
    return out
```

```python
import numpy as np
import concourse.bass as bass
import concourse.mybir as mybir
from concourse.bass_utils import run_bass_kernel_spmd

F32 = mybir.dt.float32
BF16 = mybir.dt.bfloat16
AF = mybir.ActivationFunctionType
ALU = mybir.AluOpType

NCORES = 8
D = 1024
T = 2048
HALO = 16
NB = 4
BN = 512
DFF = 2816
NG = 11
GW = 256
PRE_BLK = 12
NQ = PRE_BLK + NB
XCOLS = HALO + NQ * BN
EPS = 1e-6
WINS = (2, 4, 8, 16)

PC = {}
_o = 0
for _nm, _w in [("g_mix", 8), ("conv_w", 16), ("conv_b", 4), ("ba", 4), ("bx", 4), ("lam", 4), ("pool_b", 4),
                ("pool_s", 4), ("g_lru", 4), ("g_pool", 4), ("g_ffn", 8), ("g_fin", 8), ("invc", 64), ("pmask", 12)]:
    PC[_nm] = _o
    _o += _w
NPAR = _o

KB = 1024


class Res:
    __slots__ = ("name", "w", "r")

    def __init__(self, name):
        self.name = name
        self.w = None
        self.r = {}


class Prog:
    def __init__(self, nc):
        self.nc = nc
        self.eng = {"pe": nc.tensor, "act": nc.scalar, "dve": nc.vector, "pool": nc.gpsimd, "sp": nc.sync}
        self.sems = {}
        self.cnt = {}
        self.known = {e: {} for e in self.eng}
        self._stack = []
        for e in self.eng:
            self._mk(e)
        self.n_wait = 0
        self.n_ins = 0

    def _mk(self, key):
        cm = self.nc.semaphore("s_" + key)
        h = cm.__enter__()
        self._stack.append(cm)
        self.sems[key] = h
        self.cnt[key] = 0
        return h

    def close(self):
        for cm in reversed(self._stack):
            cm.__exit__(None, None, None)

    def _need(self, e, tok):
        if tok is None:
            return
        k, v = tok
        if k == "pe" and e == "pe":
            return
        if self.known[e].get(k, 0) < v:
            self.eng[e].wait_ge(self.sems[k], v)
            self.known[e][k] = v
            self.n_wait += 1

    def deps(self, e, reads, writes):
        for r in reads:
            self._need(e, r.w)
        for w in writes:
            self._need(e, w.w)
            for k, v in w.r.items():
                self._need(e, (k, v))

    def mark(self, tok, reads, writes):
        k, v = tok
        for r in reads:
            if r.r.get(k, 0) < v:
                r.r[k] = v
        for w in writes:
            w.w = tok
            w.r = {}

    def op(self, e, fn, reads=(), writes=(), signal=True):
        self.deps(e, reads, writes)
        ins = fn()
        self.n_ins += 1
        if signal:
            self.cnt[e] += 1
            ins.then_inc(self.sems[e], 1)
            tok = (e, self.cnt[e])
        else:
            tok = (e, self.cnt[e] + 1)
        self.mark(tok, reads, writes)
        return ins

    def dma(self, q, semkey, out, in_, reads=(), writes=(), **kw):
        if semkey not in self.sems:
            self._mk(semkey)
        self.deps(q, reads, writes)
        ins = self.eng[q].dma_start(out=out, in_=in_, **kw)
        self.cnt[semkey] += 16
        ins.then_inc(self.sems[semkey], 16)
        self.mark((semkey, self.cnt[semkey]), reads, writes)
        self.n_ins += 1
        return ins


class Rot:
    def __init__(self, views, name):
        self.v = views
        self.r = [Res(f"{name}{i}") for i in range(len(views))]
        self.i = -1

    def next(self):
        self.i = (self.i + 1) % len(self.v)
        return self.v[self.i], self.r[self.i]


STOP = None


def build_nc():
    nc = bass.Bass("TRN2", target_bir_lowering=False)
    dt_in = lambda name, shape: nc.dram_tensor(name, shape, F32, kind="ExternalInput").ap()
    xT = dt_in("xT", [128, 8, XCOLS])
    w_in_d = dt_in("w_in", [128, 8, 1536])
    w_out_d = dt_in("w_out", [128, 8, 1024])
    w1_d = dt_in("w1g", [NG, 128, 8, GW])
    w3_d = dt_in("w3g", [NG, 128, 8, GW])
    w2_d = dt_in("w2g", [NG, 128, 2, 1024])
    gates_d = dt_in("gates", [128, 2, 4, 128])
    poolw_d = dt_in("poolw", [128, 4, 128])
    par_d = dt_in("params", [128, NPAR])
    out_d = nc.dram_tensor("out", [128, 8, T], F32, kind="ExternalOutput").ap()

    ARENA = 204 * KB
    with (
        nc.sbuf_tensor("arena", [128, ARENA // 4], F32) as arena,
        nc.sbuf_tensor("misc", [128, 512], F32) as misc,
        nc.psum_tensor("ps0", [128, 512], F32) as ps0, nc.psum_tensor("ps1", [128, 512], F32) as ps1,
        nc.psum_tensor("ps2", [128, 512], F32) as ps2, nc.psum_tensor("ps3", [128, 512], F32) as ps3,
        nc.psum_tensor("ps4", [128, 512], F32) as ps4, nc.psum_tensor("ps5", [128, 512], F32) as ps5,
        nc.psum_tensor("ps6", [128, 512], F32) as ps6, nc.psum_tensor("ps7", [128, 512], F32) as ps7,
    ):
        P = Prog(nc)
        PS = [ps0, ps1, ps2, ps3, ps4, ps5, ps6, ps7]
        PSR = [Res(f"ps{i}") for i in range(8)]

        def V(off, dtype, shape):
            n = int(np.prod(shape))
            esz = 4 if dtype == F32 else 2
            nb = n * esz
            assert off % 4 == 0 and nb % 4 == 0 and off + nb <= ARENA, (off, nb)
            ap = arena[:, off // 4:(off + nb) // 4]
            if dtype != F32:
                ap = ap.bitcast(dtype)
            if len(shape) == 2:
                ap = ap.rearrange("p (a b) -> p a b", a=shape[0])
            elif len(shape) == 3:
                ap = ap.rearrange("p (a b c) -> p a b c", a=shape[0], b=shape[1])
            return ap

        R_OFF = 0
        RTILE = [[R_OFF + n * 24 * KB + i * 2 * KB for i in range(12)] for n in range(NB)]
        RRES = [[Res(f"R{n}_{i}") for i in range(12)] for n in range(NB)]
        W_IN_OFF = 96 * KB
        W_OUT_OFF = 120 * KB
        GATES_OFF = 136 * KB
        POOLW_OFF = 138 * KB
        ZERO_OFF = 139 * KB
        ONES_OFF = 141 * KB
        STR_OFF = 142 * KB
        T0_OFF = 185 * KB + 512

        w_in_sb = V(W_IN_OFF, BF16, [8, 1536])
        w_out_sb = V(W_OUT_OFF, BF16, [8, 1024])
        gates_sb = V(GATES_OFF, BF16, [2, 4, 128])
        poolw_sb = V(POOLW_OFF, BF16, [4, 128])
        zeros = V(ZERO_OFF, F32, [512])
        ones1024 = V(ONES_OFF, BF16, [128])
        ones512 = V(ONES_OFF + 256, BF16, [128])
        par = misc[:, 0:NPAR]
        DER = NPAR
        d_hba = misc[:, DER + 0:DER + 4]
        d_hbx = misc[:, DER + 4:DER + 8]
        d_hc1 = misc[:, DER + 8:DER + 12]
        d_pbs = misc[:, DER + 12:DER + 16]
        d_tmp = misc[:, DER + 16:DER + 20]
        hc = misc[:, DER + 20:DER + 24]
        pc = misc[:, DER + 24:DER + 28]
        ab = misc[:, DER + 28:DER + 36]
        hs = misc[:, DER + 36:DER + 40]
        hs_t = misc[:, DER + 40:DER + 44]
        gat = misc[:, DER + 48:DER + 48 + 64].rearrange("p (r f) -> p r f", r=8)
        assert DER + 48 + 64 <= 512
        R_par = Res("par"); R_der = Res("der"); R_hc = Res("hc"); R_pc = Res("pc"); R_ab = Res("ab")
        R_hs = Res("hs"); R_gat = Res("gat"); R_const = Res("const")
        R_win = [Res(f"win{i}") for i in range(4)]
        R_wout = [Res(f"wout{i}") for i in range(2)]
        R_gates = Res("gates"); R_poolw = Res("poolw")
        R_ccin = Res("ccin"); R_ccout = Res("ccout")

        def pcol(name, i=0, n=1):
            return par[:, PC[name] + i:PC[name] + i + n]

        o = STR_OFF
        xt_rot = Rot([V(RTILE[i // 2][6 + i % 2], F32, [512]) for i in range(8)], "xt")
        xt_rot.r = [RRES[i // 2][6 + i % 2] for i in range(8)]
        o += 4 * KB
        sq_rot = Rot([V(o + i * KB, BF16, [512]) for i in range(2)], "sq"); o += 2 * KB
        xg = V(o, BF16, [8, 512]); R_xg = [Res(f"xg{k}") for k in range(8)]; o += 8 * KB
        UW = HALO + BN
        u_lru = V(o, F32, [4, UW]); R_ulru = [Res(f"ulru{k}") for k in range(4)]; o += 4 * UW * 4
        u_pool = V(o, F32, [4, UW]); R_upool = [Res(f"upool{k}") for k in range(4)]; o += 4 * UW * 4
        rstd_rot = Rot([V(o + i * 2 * KB, F32, [512]) for i in range(2)], "rstd"); o += 4 * KB
        B_rot = Rot([V(o + i * 2 * KB, F32, [512]) for i in range(4)], "B"); o += 8 * KB
        assert o <= T0_OFF, (o, T0_OFF)
        o = T0_OFF
        E_rot = Rot([V(o + i * 2 * KB, F32, [512]) for i in range(4)], "E"); o += 8 * KB
        C_rot = Rot([V(RTILE[i][4], F32, [512]) for i in range(4)], "C")
        C_rot.r = [RRES[i][4] for i in range(4)]
        xcb_rot = Rot([V(RTILE[i][5], BF16, [512]) for i in range(4)], "xcb")
        xcb_rot.r = [RRES[i][5] for i in range(4)]
        SW = UW * 4
        S_rot = Rot([V(o + i * SW, F32, [UW]) for i in range(2)], "S"); o += 2 * SW
        pl_rot = Rot([V(o + i * KB, BF16, [512]) for i in range(2)], "pl"); o += 2 * KB
        ysq_rot = Rot([V(o + i * KB, BF16, [512]) for i in range(2)], "ysq"); o += 2 * KB
        XC7_OFF = o; o += 2 * KB
        assert o <= ARENA, (o, ARENA)
        xc_rot = Rot([V(W_OUT_OFF + i * 2 * KB, F32, [512]) for i in range(4)]
                     + [V(STR_OFF, F32, [512]), V(STR_OFF + 2 * KB, F32, [512]), V(ZERO_OFF, F32, [512]), V(XC7_OFF, F32, [512])], "xc")
        A_rot = Rot([V(W_OUT_OFF + 8 * KB + i * 2 * KB, F32, [512]) for i in range(4)], "A")
        F_views = [V(RTILE[n][10], F32, [512]) for n in range(NB)]
        F_res = [RRES[n][10] for n in range(NB)]
        yp_views = [V(RTILE[n][11], F32, [512]) for n in range(NB)]
        yp_res = [RRES[n][11] for n in range(NB)]

        def act(out, in_, func, reads, writes, scale=1.0, bias=None):
            kw = {} if bias is None else {"bias": bias}
            return P.op("act", lambda: nc.scalar.activation(out=out, in_=in_, func=func, scale=scale, **kw), reads, writes)

        def ts(e, out, in0, s1, op0, reads, writes, s2=None, op1=None):
            eng = nc.vector if e == "dve" else nc.gpsimd
            if op1 is None:
                return P.op(e, lambda: eng.tensor_scalar(out=out, in0=in0, scalar1=s1, scalar2=None, op0=op0), reads, writes)
            return P.op(e, lambda: eng.tensor_scalar(out=out, in0=in0, scalar1=s1, scalar2=s2, op0=op0, op1=op1), reads, writes)

        def tt(e, out, in0, in1, op, reads, writes):
            eng = nc.vector if e == "dve" else nc.gpsimd
            return P.op(e, lambda: eng.tensor_tensor(out=out, in0=in0, in1=in1, op=op), reads, writes)

        def stt(out, in0, scalar, in1, op0, op1, reads, writes):
            return P.op("dve", lambda: nc.vector.scalar_tensor_tensor(out=out, in0=in0, scalar=scalar, in1=in1, op0=op0, op1=op1), reads, writes)

        def mm(out, lhsT, rhs, start, stop, reads, writes, sig=False):
            return P.op("pe", lambda: nc.tensor.matmul(out, lhsT, rhs, start=start, stop=stop), reads, writes, signal=(stop or sig))

        P.dma("sp", "d_par", par, par_d, writes=[R_par])
        for i, c0_ in enumerate([0, 1024, 512]):
            P.dma("pool", f"d_win{i}", w_in_sb[:, :, c0_:c0_ + 512], w_in_d[:, :, c0_:c0_ + 512], writes=[R_win[i]], max_dma_last_dim=4096)
        P.dma("pool", "d_gates", gates_sb, gates_d, writes=[R_gates], max_dma_last_dim=4096)
        P.dma("pool", "d_poolw", poolw_sb, poolw_d, writes=[R_poolw], max_dma_last_dim=4096)
        P.op("dve", lambda: nc.vector.memset(ones1024, 1.0 / 1024.0), writes=[R_const])
        P.op("dve", lambda: nc.vector.memset(ones512, 1.0 / 512.0), writes=[R_const])
        ts("dve", d_hba, pcol("ba", 0, 4), 0.5, ALU.mult, [R_par], [R_der])
        ts("dve", d_hbx, pcol("bx", 0, 4), 0.5, ALU.mult, [R_par], [R_der])
        tt("dve", d_pbs, pcol("pool_b", 0, 4), pcol("pool_s", 0, 4), ALU.mult, [R_par], [R_der])
        act(d_tmp, pcol("lam", 0, 4), AF.Exp, [R_par], [R_der], scale=-1.0)
        act(d_tmp, d_tmp, AF.Ln, [R_der], [R_der], scale=1.0, bias=1.0)
        ts("dve", d_hc1, d_tmp, -4.0, ALU.mult, [R_der], [R_der])
        R_hcs = [Res(f"hc{i}") for i in range(4)]
        P.op("dve", lambda: nc.vector.memset(hc, 0.0), writes=R_hcs)
        P.op("dve", lambda: nc.vector.memset(pc, 0.5), writes=[R_pc])

        rstd_of = {}

        def pre(n):
            ncol = HALO if n < 0 else BN
            c0 = 0 if n < 0 else HALO + n * BN
            pss = PS[2][:, 0:ncol]
            for kt in range(8):
                xt, r_xt = xt_rot.next()
                P.dma("sp", f"d_xt{xt_rot.i}", xt[:, 0:ncol], xT[:, kt, c0:c0 + ncol], writes=[r_xt])
                sq, r_sq = sq_rot.next()
                act(sq[:, 0:ncol], xt[:, 0:ncol], AF.Square, [r_xt], [r_sq])
                mm(pss, ones1024, sq[:, 0:ncol], kt == 0, kt == 7, [r_sq, R_const], [PSR[2]], sig=True)
                ts("pool", xg[:, kt, 0:ncol], xt[:, 0:ncol], pcol("g_mix", kt), ALU.mult, [r_xt, R_par], [R_xg[kt]], s2=1.0, op1=ALU.mult)
            act(pss, pss, AF.Sqrt, [PSR[2]], [PSR[2]], bias=EPS)
            rs, r_rs = rstd_rot.next()
            P.op("dve", lambda: nc.vector.reciprocal(out=rs[:, 0:ncol], in_=pss), [PSR[2]], [r_rs])
            rstd_of[n] = (rs, r_rs)

        pu_i = [0]

        def inproj(q):
            ncol = HALO if q < 0 else BN
            dcol = 0 if q < 0 else HALO
            rs, r_rs = rstd_of[q]
            full = q >= PRE_BLK
            tiles = [0, 1, 2, 3] + ([8, 9, 10, 11, 4, 5, 6, 7] if full else ([8, 9, 10, 11] if q == PRE_BLK - 1 else []))
            E_of = {}
            for m in tiles:
                banks = [0, 1] if full else [0, 1, 7]
                b = banks[pu_i[0] % len(banks)]
                pu_i[0] += 1
                pool_halo = (m >= 8 and not full)
                cs = BN - HALO if pool_halo else 0
                nc_ = HALO if pool_halo else ncol
                pu = PS[b][:, 0:nc_]
                for kt in range(8):
                    mm(pu, w_in_sb[:, kt, m * 128:(m + 1) * 128], xg[:, kt, cs:cs + nc_], kt == 0, kt == 7,
                       [R_win[0 if m < 4 else (1 if m >= 8 else 2)], R_xg[kt]], [PSR[b]])
                if m < 4:
                    dst, rd = u_lru[:, m, dcol:dcol + ncol], R_ulru[m]
                elif m >= 8:
                    dd = 0 if pool_halo else dcol
                    dst, rd = u_pool[:, m - 8, dd:dd + nc_], R_upool[m - 8]
                else:
                    ev, rd = E_rot.next()
                    dst = ev
                    E_of[m - 4] = (ev, rd)
                    gate_E[m - 4] = (ev, rd)
                tt("dve", dst, pu, rs[:, cs:cs + nc_], ALU.mult, [PSR[b], r_rs], [rd])

        SQ_C = float(np.sqrt(0.044715))
        gate_E = {}

        def gate_block():
            for ct in range(4):
                ug, r_ug = gate_E[ct]
                act(F_views[ct], ug, AF.Square, [r_ug], [F_res[ct]], scale=SQ_C)
            for ct in range(4):
                ug, r_ug = gate_E[ct]
                stt(F_views[ct], F_views[ct], 1.0, ug, ALU.add, ALU.mult, [F_res[ct], r_ug], [F_res[ct]])
            for ct in range(4):
                act(F_views[ct], F_views[ct], AF.Tanh, [F_res[ct]], [F_res[ct]], scale=0.7978845608028654)
            for ct in range(4):
                ug, r_ug = gate_E[ct]
                stt(F_views[ct], F_views[ct], 1.0, ug, ALU.add, ALU.mult, [F_res[ct], r_ug], [F_res[ct]])

        lru_state = {}

        blk_state = {}

        def front_a(q):
            XC = [xc_rot.next() for _ in range(4)]
            for ct in range(4):
                act(XC[ct][0], u_lru[:, ct, HALO:HALO + BN], AF.Identity, [R_ulru[ct], R_par], [XC[ct][1]],
                    scale=pcol("conv_w", ct * 4 + 3), bias=pcol("conv_b", ct))
            for k in range(3):
                for ct in range(4):
                    xc, r_xc = XC[ct]
                    stt(xc, u_lru[:, ct, HALO - 3 + k:HALO - 3 + k + BN], pcol("conv_w", ct * 4 + k), xc, ALU.mult, ALU.add,
                        [R_ulru[ct], R_par, r_xc], [r_xc])
            blk_state[q] = {"XC": XC}

        def front_b(q):
            XC = blk_state[q]["XC"]
            XB = [xcb_rot.next() for _ in range(4)]
            for ct in range(4):
                act(XB[ct][0], XC[ct][0], AF.Copy, [XC[ct][1]], [XB[ct][1]])
            blk_state[q]["XB"] = XB

        def back(q):
            XC, XB = blk_state[q]["XC"], blk_state[q]["XB"]
            AA = [A_rot.next() for _ in range(4)]
            CC = [C_rot.next() for _ in range(4)]
            for pair in range(2):
                for ct in (2 * pair, 2 * pair + 1):
                    pr, pi = PS[3 + 2 * (ct % 2)], PS[4 + 2 * (ct % 2)]
                    r_pr, r_pi = PSR[3 + 2 * (ct % 2)], PSR[4 + 2 * (ct % 2)]
                    mm(pr[:, :], gates_sb[:, 0, ct, :], XB[ct][0], True, True, [R_gates, XB[ct][1]], [r_pr])
                    mm(pi[:, :], gates_sb[:, 1, ct, :], XB[ct][0], True, True, [R_gates, XB[ct][1]], [r_pi])
                for ct in (2 * pair, 2 * pair + 1):
                    pr, pi = PS[3 + 2 * (ct % 2)], PS[4 + 2 * (ct % 2)]
                    r_pr, r_pi = PSR[3 + 2 * (ct % 2)], PSR[4 + 2 * (ct % 2)]
                    act(AA[ct][0], pr[:, :], AF.Tanh, [r_pr, R_der], [AA[ct][1]], scale=0.5, bias=d_hba[:, ct:ct + 1])
                    act(CC[ct][0], pi[:, :], AF.Tanh, [r_pi, R_der], [CC[ct][1]], scale=0.5, bias=d_hbx[:, ct:ct + 1])
            for ct in range(4):
                a, r_a = AA[ct]
                act(a, a, AF.Exp, [r_a, R_der], [r_a], scale=d_hc1[:, ct:ct + 1], bias=d_hc1[:, ct:ct + 1])
            BB = [B_rot.next() for _ in range(4)]
            for ct in range(4):
                act(BB[ct][0], AA[ct][0], AF.Square, [AA[ct][1]], [BB[ct][1]])
            for ct in range(4):
                ts("pool", BB[ct][0], BB[ct][0], 1.0, ALU.min, [BB[ct][1]], [BB[ct][1]], s2=0.0, op1=ALU.max)
            for ct in range(4):
                xc, r_xc = XC[ct]
                stt(xc, CC[ct][0], 1.0, xc, ALU.add, ALU.mult, [CC[ct][1], r_xc], [r_xc])
            blk_state[q]["st"] = [(XC[ct][0], XC[ct][1], AA[ct][0], AA[ct][1], BB[ct][0], BB[ct][1]) for ct in range(4)]

        def lru_eblock(q):
            front_a(q)
            front_b(q)
            back(q)

        def lru_sblock(q):
            n = q - PRE_BLK
            lru_state = blk_state[q]["st"]
            for ct in range(4):
                xc, r_xc, a, r_a, bb, r_b = lru_state[ct]
                act(bb, bb, AF.Sqrt, [r_b], [r_b], scale=-1.0 / 16.0, bias=1.0 / 16.0)
            for ct in range(4):
                xc, r_xc, a, r_a, bb, r_b = lru_state[ct]
                tt("dve", xc, xc, bb, ALU.mult, [r_xc, r_b], [r_xc])
            for ct in range(4):
                xc, r_xc, a, r_a, bb, r_b = lru_state[ct]
                P.op("dve", lambda a=a, xc=xc, bb=bb, ct=ct: nc.vector.tensor_tensor_scan(
                    out=bb, data0=a, data1=xc, initial=hc[:, ct:ct + 1], op0=ALU.mult, op1=ALU.add),
                    [r_a, r_xc, R_hcs[ct]], [r_b])
            for ct in range(4):
                xc, r_xc, a, r_a, bb, r_b = lru_state[ct]
                if q < PRE_BLK:
                    ts("dve", hc[:, ct:ct + 1], bb[:, BN - 1:BN], pcol("pmask", q), ALU.mult, [r_b, R_par], [R_hcs[ct]])
                else:
                    P.op("dve", lambda bb=bb, ct=ct: nc.vector.tensor_copy(out=hc[:, ct:ct + 1], in_=bb[:, BN - 1:BN]), [r_b], [R_hcs[ct]])
            if q >= PRE_BLK:
                for ct in range(4):
                    xc, r_xc, a, r_a, bb, r_b = lru_state[ct]
                    tt("pool", V(RTILE[n][ct], F32, [512]), bb, F_views[ct], ALU.mult, [r_b, F_res[ct]], [RRES[n][ct]])

        def pool_epart(n):
            for gi in range(4):
                w = WINS[gi]
                U = u_pool[:, gi, :]
                rU = R_upool[gi]
                src, r_src = U, rU
                sh = 1
                lo = 1
                while sh < w:
                    s, r_s = S_rot.next()
                    tt("dve", s[:, lo:UW], src[:, lo:UW], src[:, lo - sh:UW - sh], ALU.add, [r_src], [r_s])
                    src, r_src = s, r_s
                    sh *= 2
                    lo = 2 * sh - 1 if sh < w else lo
                    lo = min(lo, HALO)
                pl, r_pl = pl_rot.next()
                stt(pl, src[:, HALO:UW], 1.0 / w, U[:, HALO:UW], ALU.mult, ALU.subtract, [r_src, rU], [r_pl])
                if n == 0:
                    ic = par[:, PC["invc"] + gi * 16:PC["invc"] + gi * 16 + 16]
                    s2, r_s2 = S_rot.next()
                    tt("dve", s2[:, 0:16], src[:, HALO:HALO + 16], ic, ALU.mult, [r_src, R_par], [r_s2])
                    tt("dve", pl[:, 0:16], s2[:, 0:16], U[:, HALO:HALO + 16], ALU.subtract, [r_s2, rU], [r_pl])
                mm(PS[7][:, :], poolw_sb[:, gi, :], pl, True, True, [R_poolw, r_pl], [PSR[7]])
                yp, r_yp = yp_views[gi], yp_res[gi]
                act(yp, PS[7][:, :], AF.Identity, [PSR[7], R_par, R_der], [r_yp], scale=pcol("pool_s", gi), bias=d_pbs[:, gi:gi + 1])
                ysq, r_ysq = ysq_rot.next()
                act(ysq, yp, AF.Square, [r_yp], [r_ysq])
                mm(PS[2][:, :], ones512, ysq, gi == 0, gi == 3, [r_ysq, R_const], [PSR[2]], sig=True)

        def pool_spart(n):
            act(PS[2][:, :], PS[2][:, :], AF.Sqrt, [PSR[2]], [PSR[2]], bias=EPS)
            rs, r_rs = rstd_rot.next()
            P.op("dve", lambda: nc.vector.reciprocal(out=rs, in_=PS[2][:, :]), [PSR[2]], [r_rs])
            ynp = V(RTILE[n][8], BF16, [4, 512])
            for gi in range(4):
                stt(ynp[:, gi, :], yp_views[gi], pcol("g_pool", gi), rs, ALU.mult, ALU.mult,
                    [yp_res[gi], R_par, r_rs], [RRES[n][8 + gi // 2]])

        def halo_copy(q):
            for ct in range(4):
                P.op("pool", lambda ct=ct: nc.gpsimd.tensor_copy(out=u_lru[:, ct, 0:HALO], in_=u_lru[:, ct, BN:BN + HALO]), [R_ulru[ct]], [R_ulru[ct]])
                if q >= PRE_BLK:
                    P.op("pool", lambda ct=ct: nc.gpsimd.tensor_copy(out=u_pool[:, ct, 0:HALO], in_=u_pool[:, ct, BN:BN + HALO]), [R_upool[ct]], [R_upool[ct]])

        pre(-1)
        inproj(-1)
        pre(0)
        inproj(0)
        front_a(0)
        front_b(0)
        halo_copy(0)
        pre(1)
        for q in range(PRE_BLK):
            if q + 1 < PRE_BLK:
                inproj(q + 1)
                front_a(q + 1)
            back(q)
            lru_sblock(q)
            if q + 1 < PRE_BLK:
                front_b(q + 1)
                halo_copy(q + 1)
            if q + 2 <= PRE_BLK:
                pre(q + 2)
        inproj(PRE_BLK)
        front_a(PRE_BLK)
        front_b(PRE_BLK)
        pre(PRE_BLK + 1)
        for q in range(PRE_BLK, NQ):
            n = q - PRE_BLK
            gate_block()
            pool_epart(n)
            halo_copy(q)
            if q + 1 < NQ:
                inproj(q + 1)
                front_a(q + 1)
            back(q)
            lru_sblock(q)
            pool_spart(n)
            if q + 1 < NQ:
                front_b(q + 1)
            if q + 2 < NQ:
                pre(q + 2)

        def inherit(new_res, old_res):
            for rr in new_res:
                for s in old_res:
                    if s.w is not None:
                        k, v = s.w
                        if rr.r.get(k, 0) < v:
                            rr.r[k] = v
                    for k, v in s.r.items():
                        if rr.r.get(k, 0) < v:
                            rr.r[k] = v

        wout_res_half = [xc_rot.r[0:4], A_rot.r]
        for i in range(2):
            inherit([R_wout[i]], wout_res_half[i])
            P.dma("pool", f"d_wout{i}", w_out_sb[:, 4 * i:4 * i + 4, :], w_out_d[:, 4 * i:4 * i + 4, :],
                  writes=[R_wout[i]], max_dma_last_dim=4096)
        if STOP == 'exch':
            P.close(); return nc
        a1_stream_res = (xc_rot.r[4:8] + xt_rot.r + sq_rot.r + R_xg + R_ulru + R_upool + rstd_rot.r + B_rot.r)
        a1_t0_res = E_rot.r + C_rot.r + xcb_rot.r + S_rot.r + pl_rot.r + ysq_rot.r

        o = STR_OFF
        ynl_rot = Rot([V(o + i * 4 * KB, BF16, [4, 512]) for i in range(2)]
                      + [V(STR_OFF + 22 * KB + i * 4 * KB, BF16, [4, 512]) for i in range(2)], "ynl"); o += 8 * KB
        x2_rot = Rot([V(o + i * 2 * KB, F32, [512]) for i in range(4)], "x2"); o += 8 * KB
        sq2_rot = Rot([V(o + i * KB, BF16, [512]) for i in range(2)], "sq2"); o += 2 * KB
        rstd2_rot = Rot([V(o + i * 2 * KB, F32, [512]) for i in range(2)], "rstd2"); o += 4 * KB
        assert o <= T0_OFF
        barrier_srcs = a1_stream_res
        for rr in ynl_rot.r + x2_rot.r + sq2_rot.r + rstd2_rot.r:
            for s in barrier_srcs:
                if s.w is not None:
                    k, v = s.w
                    if rr.r.get(k, 0) < v:
                        rr.r[k] = v
                for k, v in s.r.items():
                    if rr.r.get(k, 0) < v:
                        rr.r[k] = v

        WS = [W_IN_OFF, W_IN_OFF + 12 * KB, T0_OFF]
        w1s = [V(WS[i], BF16, [8, GW]) for i in range(3)]
        w3s = [V(WS[i] + 4 * KB, BF16, [8, GW]) for i in range(3)]
        w2s = [V(WS[i] + 8 * KB, BF16, [2, 1024]) for i in range(3)]
        R_w1 = [Res(f"w1s{i}") for i in range(3)]
        R_w3 = [Res(f"w3s{i}") for i in range(3)]
        R_w2 = [Res(f"w2s{i}") for i in range(3)]
        for RW in (R_w1, R_w3, R_w2):
            inherit(RW[0:2], R_win)
            inherit(RW[2:3], a1_t0_res)

        def load_group(g):
            s = g % 3
            P.dma("pool", f"d_w1_{s}", w1s[s], w1_d[g], writes=[R_w1[s]], max_dma_last_dim=4096)
            P.dma("pool", f"d_w3_{s}", w3s[s], w3_d[g], writes=[R_w3[s]], max_dma_last_dim=4096)
            P.dma("pool", f"d_w2_{s}", w2s[s], w2_d[g], writes=[R_w2[s]], max_dma_last_dim=4096)

        if STOP is None or not STOP.startswith('a2c'):
            load_group(0)
            load_group(1)
            load_group(2)
        if STOP == 'ldg':
            for k in ["d_wout0", "d_wout1"] + [f"d_w{t}_{s_}" for t in (1, 2, 3) for s_ in range(3)]:
                nc.sync.wait_ge(P.sems[k], P.cnt[k])
            P.close(); return nc

        po_banks = [0, 1, 3, 4]
        po_i = [0]

        class _StopA2(Exception):
            pass

        a2_state = {}

        def a2_p1(n):
            yv = [V(RTILE[n][ct], F32, [512]) for ct in range(4)]
            for ct in range(4):
                sq, r_sq = sq2_rot.next()
                act(sq, yv[ct], AF.Square, [RRES[n][ct]], [r_sq])
                mm(PS[2][:, :], ones512, sq, ct == 0, ct == 3, [r_sq, R_const], [PSR[2]], sig=True)
            act(PS[2][:, :], PS[2][:, :], AF.Sqrt, [PSR[2]], [PSR[2]], bias=EPS)
            rs, r_rs = rstd2_rot.next()
            P.op("dve", lambda: nc.vector.reciprocal(out=rs, in_=PS[2][:, :]), [PSR[2]], [r_rs])
            ynl, r_ynl = ynl_rot.next()
            for ct in range(4):
                stt(ynl[:, ct, :], yv[ct], pcol("g_lru", ct), rs, ALU.mult, ALU.mult, [RRES[n][ct], R_par, r_rs], [r_ynl])
            a2_state[n] = (ynl, r_ynl)

        def a2_p2(n):
            ynl, r_ynl = a2_state[n]
            ynp = V(RTILE[n][8], BF16, [4, 512])
            hres = V(RTILE[n][0], F32, [8, 512])
            c0 = HALO + (PRE_BLK + n) * BN
            for m in range(8):
                b = po_banks[po_i[0] % 4]
                po_i[0] += 1
                for kt in range(8):
                    rhs = ynl[:, kt, :] if kt < 4 else ynp[:, kt - 4, :]
                    rr = r_ynl if kt < 4 else RRES[n][8 + (kt - 4) // 2]
                    mm(PS[b][:, :], w_out_sb[:, kt, m * 128:(m + 1) * 128], rhs, kt == 0, kt == 7, [R_wout[kt // 4], rr], [PSR[b]])
                x2, r_x2 = x2_rot.next()
                P.dma("sp", f"d_x2_{x2_rot.i}", x2, xT[:, m, c0:c0 + BN], writes=[r_x2])
                tt("dve", hres[:, m, :], PS[b][:, :], x2, ALU.add, [PSR[b], r_x2], [RRES[n][m]])

        def a2_p3(n):
            hres = V(RTILE[n][0], F32, [8, 512])
            for m in range(8):
                sq, r_sq = sq2_rot.next()
                act(sq, hres[:, m, :], AF.Square, [RRES[n][m]], [r_sq])
                mm(PS[2][:, :], ones1024, sq, m == 0, m == 7, [r_sq, R_const], [PSR[2]], sig=True)
            act(PS[2][:, :], PS[2][:, :], AF.Sqrt, [PSR[2]], [PSR[2]], bias=EPS)
            rs, r_rs = rstd2_rot.next()
            P.op("dve", lambda: nc.vector.reciprocal(out=rs, in_=PS[2][:, :]), [PSR[2]], [r_rs])
            hffn = V(RTILE[n][8], BF16, [8, 512])
            for m in range(8):
                stt(hffn[:, m, :], hres[:, m, :], pcol("g_ffn", m), rs, ALU.mult, ALU.mult,
                    [RRES[n][m], R_par, r_rs], [RRES[n][8 + m // 2]])

        for n in range(NB):
            a2_p1(n)
        a2_p2(0)
        for n in range(NB):
            if n + 1 < NB:
                a2_p2(n + 1)
            a2_p3(n)

        if STOP in ('a2', 'a2c'):
            P.close(); return nc
        ff = [V(W_OUT_OFF + i * 8 * KB, BF16, [2, 4, 512]) for i in range(2)]
        R_ff = [[[Res(f"ff{i}_{j}_{n}") for n in range(NB)] for j in range(2)] for i in range(2)]
        for i in range(2):
            for j in range(2):
                inherit(R_ff[i][j], R_wout)
        o = STR_OFF
        sl_rot = Rot([V(o + i * 2 * KB, F32, [512]) for i in range(3)], "sl"); o += 6 * KB
        ot_rot = Rot([V(o + i * 2 * KB, F32, [512]) for i in range(4)], "ot"); o += 8 * KB
        sq3_rot = Rot([V(o + i * KB, BF16, [512]) for i in range(2)], "sq3"); o += 2 * KB
        rstd3_rot = Rot([V(o + i * 2 * KB, F32, [512]) for i in range(2)], "rstd3"); o += 4 * KB
        a2_res = ynl_rot.r + x2_rot.r + sq2_rot.r + rstd2_rot.r
        inherit(sl_rot.r + ot_rot.r + sq3_rot.r + rstd3_rot.r, a2_res)

        pa_i = [0]
        pd_i = [0]
        out_sems = []

        def up(g, n, js=(0, 1)):
            s = g % 3
            hffn = V(RTILE[n][8], BF16, [8, 512])
            for j in js:
                ia = pa_i[0] % 2
                pa_i[0] += 1
                pa, r_pa = PS[0 + ia], PSR[0 + ia]
                pb, r_pb = PS[3 + ia], PSR[3 + ia]
                for kt in range(8):
                    mm(pa[:, :], w1s[s][:, kt, j * 128:(j + 1) * 128], hffn[:, kt, :], kt == 0, kt == 7,
                       [R_w1[s], RRES[n][8 + kt // 2]], [r_pa])
                for kt in range(8):
                    mm(pb[:, :], w3s[s][:, kt, j * 128:(j + 1) * 128], hffn[:, kt, :], kt == 0, kt == 7,
                       [R_w3[s], RRES[n][8 + kt // 2]], [r_pb])
                sl, r_sl = sl_rot.next()
                act(sl, pa[:, :], AF.Silu, [r_pa], [r_sl])
                tt("dve", ff[g % 2][:, j, n, :], pb[:, :], sl, ALU.mult, [r_pb, r_sl], [R_ff[g % 2][j][n]])

        def down(g, n, last, ms=range(8)):
            s = g % 3
            hres = V(RTILE[n][0], F32, [8, 512])
            for m in ms:
                ib = (5, 6, 7)[pd_i[0] % 3]
                pd_i[0] += 1
                for j in range(2):
                    mm(PS[ib][:, :], w2s[s][:, j, m * 128:(m + 1) * 128], ff[g % 2][:, j, n, :], j == 0, j == 1,
                       [R_w2[s], R_ff[g % 2][j][n]], [PSR[ib]])
                tt("dve", hres[:, m, :], PS[ib][:, :], hres[:, m, :], ALU.add, [PSR[ib], RRES[n][m]], [RRES[n][m]])
            if last and 7 in ms:
                final_norm(n)

        def final_norm(n):
            hres = V(RTILE[n][0], F32, [8, 512])
            for m in range(8):
                sq, r_sq = sq3_rot.next()
                act(sq, hres[:, m, :], AF.Square, [RRES[n][m]], [r_sq])
                mm(PS[2][:, :], ones1024, sq, m == 0, m == 7, [r_sq, R_const], [PSR[2]], sig=True)
            act(PS[2][:, :], PS[2][:, :], AF.Sqrt, [PSR[2]], [PSR[2]], bias=EPS)
            rs, r_rs = rstd3_rot.next()
            P.op("dve", lambda: nc.vector.reciprocal(out=rs, in_=PS[2][:, :]), [PSR[2]], [r_rs])
            for m in range(8):
                ot, r_ot = ot_rot.next()
                stt(ot, hres[:, m, :], pcol("g_fin", m), rs, ALU.mult, ALU.mult, [RRES[n][m], R_par, r_rs], [r_ot])
                key = f"d_out{ot_rot.i}"
                P.dma("sp", key, out_d[:, m, n * BN:(n + 1) * BN], ot, reads=[r_ot])
                if key not in out_sems:
                    out_sems.append(key)

        for g in range(NG):
            for n in range(NB):
                up(g, n, (0,))
                if g > 0:
                    down(g - 1, n, False, range(0, 4))
                up(g, n, (1,))
                if g > 0:
                    down(g - 1, n, False, range(4, 8))
            if g > 0 and g + 2 < NG:
                load_group(g + 2)
        for n in range(NB):
            down(NG - 1, n, True)

        for key in out_sems:
            nc.sync.wait_ge(P.sems[key], P.cnt[key])
        print(f"[kernel] instructions={P.n_ins} waits={P.n_wait}")
        P.close()
    return nc


def _host_layout(inp):
    f = lambda a: np.ascontiguousarray(np.asarray(a, dtype=np.float32))
    x = f(inp["x"])
    shared = {}
    shared["w_in"] = f(f(inp["w_in"])[0].reshape(8, 128, 1536).transpose(1, 0, 2))
    shared["w_out"] = f(f(inp["w_out"])[0].reshape(8, 128, 1024).transpose(1, 0, 2))
    shared["w1g"] = f(f(inp["ffn_w1"])[0].reshape(8, 128, NG, GW).transpose(2, 1, 0, 3))
    shared["w3g"] = f(f(inp["ffn_w3"])[0].reshape(8, 128, NG, GW).transpose(2, 1, 0, 3))
    shared["w2g"] = f(f(inp["ffn_w2"])[0].reshape(NG, 2, 128, 1024).transpose(0, 2, 1, 3))
    gates = np.zeros((128, 2, 4, 128), np.float32)
    for gi, nm in enumerate(["gate_a_w", "gate_x_w"]):
        w = f(inp[nm])[0]
        for h in range(8):
            ct, hh = divmod(h, 2)
            gates[hh * 64:(hh + 1) * 64, gi, ct, hh * 64:(hh + 1) * 64] = w[h]
    shared["gates"] = gates
    shared["poolw"] = f(f(inp["pool_w"])[0].transpose(1, 0, 2))
    par = np.zeros((128, NPAR), np.float32)
    col = lambda v, nt: f(v).reshape(nt, 128).T
    par[:, PC["g_mix"]:PC["g_mix"] + 8] = col(f(inp["norm_mix_g"])[0], 8)
    cw = f(inp["conv_w"])[0]
    for ct in range(4):
        for k in range(4):
            par[:, PC["conv_w"] + ct * 4 + k] = cw[k, ct * 128:(ct + 1) * 128]
    par[:, PC["conv_b"]:PC["conv_b"] + 4] = col(f(inp["conv_b"])[0], 4)
    par[:, PC["ba"]:PC["ba"] + 4] = col(f(inp["gate_a_b"])[0], 4)
    par[:, PC["bx"]:PC["bx"] + 4] = col(f(inp["gate_x_b"])[0], 4)
    par[:, PC["lam"]:PC["lam"] + 4] = col(f(inp["lru_lambda"])[0], 4)
    par[:, PC["pool_b"]:PC["pool_b"] + 4] = col(f(inp["pool_b"])[0], 4)
    par[:, PC["pool_s"]:PC["pool_s"] + 4] = col(f(inp["pool_scale"])[0], 4)
    par[:, PC["g_lru"]:PC["g_lru"] + 4] = col(f(inp["norm_lru_g"])[0], 4)
    par[:, PC["g_pool"]:PC["g_pool"] + 4] = col(f(inp["norm_pool_g"])[0], 4)
    par[:, PC["g_ffn"]:PC["g_ffn"] + 8] = col(f(inp["norm_ffn_g"])[0], 8)
    par[:, PC["g_fin"]:PC["g_fin"] + 8] = col(f(inp["final_norm_g"]), 8)
    in_maps = []
    for c in range(NCORES):
        b, k = divmod(c, 4)
        xs = np.zeros((XCOLS, D), np.float32)
        lo = k * T - PRE_BLK * BN - HALO
        src0 = max(lo, 0)
        xs[src0 - lo:] = x[b, src0:(k + 1) * T]
        xTc = f(xs.T.reshape(8, 128, XCOLS).transpose(1, 0, 2))
        p = par.copy()
        for gi, w in enumerate(WINS):
            for t in range(16):
                cnt = min(t + 1, w) if k == 0 else w
                p[:, PC["invc"] + gi * 16 + t] = 1.0 / cnt
        for q in range(PRE_BLK):
            p[:, PC["pmask"] + q] = 1.0 if (k * T - PRE_BLK * BN + q * BN) >= 0 else 0.0
        m = dict(shared)
        m["xT"] = xTc
        m["params"] = p
        in_maps.append(m)
    return in_maps


_NC_CACHE = {}


def kernel(**inputs):
    in_maps = _host_layout(inputs)
    if "nc" not in _NC_CACHE:
        _NC_CACHE["nc"] = build_nc()
    nc = _NC_CACHE["nc"]
    res = run_bass_kernel_spmd(nc, in_maps, core_ids=list(range(NCORES)))
    out = np.empty((2, 4 * T, D), np.float32)
    for c in range(NCORES):
        b, k = divmod(c, 4)
        o = np.asarray(res.results[c]["out"])
        out[b, k * T:(k + 1) * T, :] = o.transpose(2, 1, 0).reshape(T, D)
    return out
```
